# Optimizing a Trainium2 kernel written in Bass

```python
import jax, jax.numpy as jnp
from jax import lax
import numpy as np

D_MODEL = 1024
BATCH = 16
SEQ = 256
DEPTH = 2
DEC_BATCH = 4
DEC_SEQ = 2048
PAST_LEN = 512

GRID_W = 64
HEAD_DIM = 64
H_A = 8
KVH_A = 2
H_B = 4
H_C = 4
WINDOW = 128
BLOCK = 128
NA_ROWS = 8
NA_COLS = 16
ROPE_BASE = 10000.0
N_EXPERTS = 16
CAPACITY_FACTOR = 2
D_FF = 1024
LN_EPS = 1e-6

kernel_name = "hybrid_diffusion_parallel_heads_step"

F32 = jnp.float32


def _split_sizes():
    qa, kva = H_A * HEAD_DIM, KVH_A * HEAD_DIM
    b, c = H_B * HEAD_DIM, H_C * HEAD_DIM
    return [qa, kva, kva, b, b, b, c, c, c, c]


def _layernorm(x, g, b):
    xf = x.astype(F32)
    mu = xf.mean(-1, keepdims=True)
    var = jnp.square(xf - mu).mean(-1, keepdims=True)
    return ((xf - mu) * lax.rsqrt(var + LN_EPS) * g.astype(F32) + b.astype(F32)).astype(x.dtype)


def _modulation(cond, w, b):
    m = jax.nn.silu(cond) @ w + b
    return [p[:, None, :] for p in jnp.split(m, 6, axis=-1)]


def _project(h, w_in):
    bsz, L = h.shape[:2]
    pts = np.cumsum(_split_sizes())[:-1].tolist()
    qa, ka, va, qb, kb, vb, qc, kc, vc, gc = jnp.split(h @ w_in, pts, axis=-1)
    hd = lambda t, n: t.reshape(bsz, L, n, HEAD_DIM)
    return (hd(qa, H_A), hd(ka, KVH_A), hd(va, KVH_A), hd(qb, H_B), hd(kb, H_B), hd(vb, H_B),
            hd(qc, H_C), hd(kc, H_C), hd(vc, H_C), gc)


def _axial_rope(x):
    L = x.shape[1]
    t = np.arange(L)
    row, col = t // GRID_W, t % GRID_W
    half = HEAD_DIM // 2
    inv = 1.0 / (ROPE_BASE ** (np.arange(0, half, 2) / half))

    def rot(xp, pos):
        ang = jnp.asarray(pos[:, None] * inv[None, :], F32)
        cos, sin = jnp.cos(ang)[None, :, None, :], jnp.sin(ang)[None, :, None, :]
        x1, x2 = jnp.split(xp.astype(F32), 2, axis=-1)
        return jnp.concatenate([x1 * cos - x2 * sin, x1 * sin + x2 * cos], -1)

    xr, xc = jnp.split(x, 2, axis=-1)
    return jnp.concatenate([rot(xr, row), rot(xc, col)], -1).astype(x.dtype)


def _to_blocks(x):
    bsz, L = x.shape[:2]
    return jnp.moveaxis(x.reshape(bsz, L // BLOCK, BLOCK, *x.shape[2:]), 1, 0)


def _from_blocks(x):
    x = jnp.moveaxis(x, 0, 1)
    return x.reshape(x.shape[0], -1, *x.shape[3:])


def _gqa_attn(q, k, v, mask=None, sink=None):
    bsz, Q, H, d = q.shape
    kvh = k.shape[2]
    g = H // kvh
    qg = q.reshape(bsz, Q, kvh, g, d)
    s = jnp.einsum('bqhgd,bkhd->bhgqk', qg, k).astype(F32) * (d ** -0.5)
    if mask is not None:
        s = jnp.where(mask, s, -jnp.inf)
    m = s.max(-1, keepdims=True)
    if sink is not None:
        sk = sink.astype(F32).reshape(kvh, g, 1, 1)
        m = jnp.maximum(m, sk)
    p = jnp.exp(s - m)
    den = p.sum(-1, keepdims=True)
    if sink is not None:
        den = den + jnp.exp(sk - m)
    o = jnp.einsum('bhgqk,bkhd->bqhgd', (p / den).astype(v.dtype), v)
    return o.reshape(bsz, Q, H, d)


def _ctx_attn(q, k, v, sink=None):
    out = lax.map(lambda qb: _gqa_attn(qb, k, v, None, sink), _to_blocks(q))
    return _from_blocks(out)


def _window_attn_latent(q, k, v, kc, vc, sink):
    bsz, L = q.shape[:2]
    nb = L // BLOCK
    Lc = kc.shape[1]

    def windows(t):
        tb = jnp.pad(t, [(0, 0), (BLOCK, BLOCK), (0, 0), (0, 0)]).reshape(bsz, nb + 2, BLOCK, *t.shape[2:])
        return jnp.moveaxis(jnp.concatenate([tb[:, :-2], tb[:, 1:-1], tb[:, 2:]], axis=2), 1, 0)

    qi = np.arange(nb)[:, None, None] * BLOCK + np.arange(BLOCK)[None, :, None]
    kj = np.arange(nb)[:, None, None] * BLOCK - BLOCK + np.arange(3 * BLOCK)[None, None, :]
    band = (np.abs(qi - kj) <= WINDOW) & (kj >= 0) & (kj < L)
    full = jnp.asarray(np.concatenate([band, np.ones((nb, BLOCK, Lc), bool)], -1))

    def blk(args):
        qb, kb, vb, mb = args
        return _gqa_attn(qb, jnp.concatenate([kb, kc], 1), jnp.concatenate([vb, vc], 1), mb, sink)

    out = lax.map(blk, (_to_blocks(q), windows(k), windows(v), full))
    return _from_blocks(out)


def _na_indices(L):
    rows = L // GRID_W
    kh, kw = min(NA_ROWS, rows), min(NA_COLS, GRID_W)
    t = np.arange(L)
    r, c = t // GRID_W, t % GRID_W
    rs = np.clip(r - kh // 2, 0, rows - kh)
    cs = np.clip(c - kw // 2, 0, GRID_W - kw)
    kr = rs[:, None, None] + np.arange(kh)[None, :, None]
    kc = cs[:, None, None] + np.arange(kw)[None, None, :]
    kr, kc = np.broadcast_arrays(kr, kc)
    idx = (kr * GRID_W + kc).reshape(L, kh * kw)
    dr = (kr - r[:, None, None]).reshape(L, -1) + NA_ROWS - 1
    dc = (kc - c[:, None, None]).reshape(L, -1) + NA_COLS - 1
    return idx, dr, dc


def _na_latent(q, k, v, kc, vc, rpb):
    bsz, L, H, d = q.shape
    nb = L // BLOCK
    idx, dr, dc = _na_indices(L)
    K = idx.shape[1]
    bias = jnp.moveaxis(rpb[:, dr, dc].reshape(H, nb, BLOCK, K), 1, 0)
    scale = d ** -0.5

    def blk(args):
        qb, ib, bb = args
        kn, vn = k[:, ib], v[:, ib]
        s_nb = jnp.einsum('bqhd,bqkhd->bhqk', qb, kn).astype(F32) * scale + bb.astype(F32)
        s_cx = jnp.einsum('bqhd,bkhd->bhqk', qb, kc).astype(F32) * scale
        p = jax.nn.softmax(jnp.concatenate([s_nb, s_cx], -1), axis=-1).astype(v.dtype)
        return (jnp.einsum('bhqk,bqkhd->bqhd', p[..., :K], vn)
                + jnp.einsum('bhqk,bkhd->bqhd', p[..., K:], vc))

    out = lax.map(blk, (_to_blocks(q), jnp.asarray(idx.reshape(nb, BLOCK, K), jnp.int32), bias))
    return _from_blocks(out)


def _retention_dir(q, k, v, log_gamma, s0):
    j = np.arange(BLOCK).astype(np.float32)
    rel = j[:, None] - j[None, :]
    lg = log_gamma.astype(F32)
    dmat = jnp.where(jnp.asarray(rel >= 0), jnp.exp(lg[:, None, None] * np.maximum(rel, 0.0)), 0.0)
    xi = jnp.exp(lg[None, :] * (j[:, None] + 1.0))
    zeta = jnp.exp(lg[None, :] * (BLOCK - 1.0 - j)[:, None])
    g_chunk = jnp.exp(lg * BLOCK)

    def step(S, args):
        qc, kc, vc = args
        inner = jnp.einsum('bhij,bjhe->bihe', jnp.einsum('bihd,bjhd->bhij', qc, kc) * dmat, vc)
        cross = jnp.einsum('bihd,bhde->bihe', qc, S) * xi[None, :, :, None]
        S = S * g_chunk[None, :, None, None] + jnp.einsum('bjhd,bjhe->bhde', kc * zeta[None, :, :, None], vc)
        return S, inner + cross

    S, out = lax.scan(step, s0.astype(F32), (_to_blocks(q), _to_blocks(k), _to_blocks(v)))
    return _from_blocks(out), S


def _retention(q, k, v, lg, s0_f, s0_b):
    qf, kf, vf = q.astype(F32), k.astype(F32) * (HEAD_DIM ** -0.5), v.astype(F32)
    of, sf = _retention_dir(qf, kf, vf, lg[0], s0_f)
    ob, sb = _retention_dir(qf[:, ::-1], kf[:, ::-1], vf[:, ::-1], lg[1], s0_b)
    return of + ob[:, ::-1], sf, sb


def _retention_out(o, g, gn_w):
    mu = o.mean(-1, keepdims=True)
    var = jnp.square(o - mu).mean(-1, keepdims=True)
    on = ((o - mu) * lax.rsqrt(var + LN_EPS)).reshape(*g.shape) * gn_w.astype(F32)
    return (jax.nn.silu(g.astype(F32)) * on).astype(g.dtype)


def _mixer_context(h, w_in, w_out, sink, lg, gn_w):
    bsz, L = h.shape[:2]
    qa, ka, va, qb, kb, vb, qc, kc, vc, gc = _project(h, w_in)
    oa = _ctx_attn(qa, ka, va, sink)
    ob = _ctx_attn(qb, kb, vb)
    z = jnp.zeros((bsz, H_C, HEAD_DIM, HEAD_DIM), F32)
    oc, sf, sb = _retention(qc, kc, vc, lg, z, z)
    cat = jnp.concatenate([oa.reshape(bsz, L, -1), ob.reshape(bsz, L, -1), _retention_out(oc, gc, gn_w)], -1)
    return cat @ w_out, (ka, va, kb, vb, jnp.stack([sf, sb], 1))


def _mixer_latent(h, cak, cav, cbk, cbv, cst, w_in, w_out, sink, rpb, lg, gn_w):
    bsz, L = h.shape[:2]
    qa, ka, va, qb, kb, vb, qc, kc, vc, gc = _project(h, w_in)
    oa = _window_attn_latent(_axial_rope(qa), _axial_rope(ka), va, cak, cav, sink)
    ob = _na_latent(qb, kb, vb, cbk, cbv, rpb)
    oc, _, _ = _retention(qc, kc, vc, lg, cst[:, 0], cst[:, 1])
    cat = jnp.concatenate([oa.reshape(bsz, L, -1), ob.reshape(bsz, L, -1), _retention_out(oc, gc, gn_w)], -1)
    return cat @ w_out


def _expert_choice(h, w_router, w_gate_up, w_down):
    n = h.shape[1]
    cap = CAPACITY_FACTOR * n // N_EXPERTS
    aff = jax.nn.softmax((h @ w_router).astype(F32), axis=-1)
    gate, idx = lax.top_k(jnp.swapaxes(aff, 1, 2), cap)

    def per_request(hb, ib, gb):
        xe = hb[ib]
        a, b = jnp.split(jnp.einsum('ecd,edf->ecf', xe, w_gate_up), 2, axis=-1)
        ye = jnp.einsum('ecf,efd->ecd', jax.nn.silu(a) * b, w_down) * gb[..., None].astype(hb.dtype)
        return jnp.zeros_like(hb).at[ib.reshape(-1)].add(ye.reshape(-1, hb.shape[-1]))

    return jax.vmap(per_request)(h, idx, gate)


def setup_inputs(seed: int = 0) -> dict:
    key = jax.random.key(seed)
    ks = jax.random.split(key, 26)
    beta = (8 * DEPTH) ** -0.25
    d_in = sum(_split_sizes())
    nrm = lambda k, shape, s=1.0: jax.random.normal(k, shape, F32) * s
    dec0 = jnp.asarray(np.log(-np.log(1.0 - 2.0 ** (-5.0 - np.arange(H_C)))), F32)
    return {
        "x_prompt": nrm(ks[0], (BATCH, SEQ, D_MODEL)),
        "x_sample": nrm(ks[1], (DEC_BATCH, DEC_SEQ, D_MODEL)),
        "cache_attn_a_k": nrm(ks[2], (DEC_BATCH, DEPTH, PAST_LEN, KVH_A, HEAD_DIM)),
        "cache_attn_a_v": nrm(ks[3], (DEC_BATCH, DEPTH, PAST_LEN, KVH_A, HEAD_DIM)),
        "cache_attn_b_k": nrm(ks[4], (DEC_BATCH, DEPTH, PAST_LEN, H_B, HEAD_DIM)),
        "cache_attn_b_v": nrm(ks[5], (DEC_BATCH, DEPTH, PAST_LEN, H_B, HEAD_DIM)),
        "state_ret": nrm(ks[6], (DEC_BATCH, DEPTH, 2, H_C, HEAD_DIM, HEAD_DIM)),
        "c": nrm(ks[7], (DEC_BATCH, D_MODEL)),
        "c_ctx": nrm(ks[8], (D_MODEL,)),
        "w_ada": nrm(ks[9], (DEPTH, D_MODEL, 6 * D_MODEL), 0.5 * D_MODEL ** -0.5),
        "b_ada": nrm(ks[10], (DEPTH, 6 * D_MODEL), 0.01),
        "w_in": nrm(ks[11], (DEPTH, D_MODEL, d_in), D_MODEL ** -0.5),
        "w_out": nrm(ks[12], (DEPTH, D_MODEL, D_MODEL), beta * D_MODEL ** -0.5),
        "attn_sink": nrm(ks[13], (DEPTH, H_A), 0.5),
        "na_rpb": nrm(ks[14], (DEPTH, H_B, 2 * NA_ROWS - 1, 2 * NA_COLS - 1), 0.5),
        "ret_decay": jnp.broadcast_to(dec0, (DEPTH, 2, H_C)) + nrm(ks[15], (DEPTH, 2, H_C), 0.05),
        "ret_gn": 1.0 + nrm(ks[16], (DEPTH, H_C * HEAD_DIM), 0.02),
        "ln1_g": 1.0 + nrm(ks[17], (DEPTH, D_MODEL), 0.02),
        "ln1_b": nrm(ks[18], (DEPTH, D_MODEL), 0.01),
        "ln2_g": 1.0 + nrm(ks[19], (DEPTH, D_MODEL), 0.02),
        "ln2_b": nrm(ks[20], (DEPTH, D_MODEL), 0.01),
        "w_router": nrm(ks[21], (DEPTH, D_MODEL, N_EXPERTS), D_MODEL ** -0.5),
        "w_gate_up": nrm(ks[22], (DEPTH, N_EXPERTS, D_MODEL, 2 * D_FF), D_MODEL ** -0.5),
        "w_down": nrm(ks[23], (DEPTH, N_EXPERTS, D_FF, D_MODEL), beta * D_FF ** -0.5),
    }


def reference(x_prompt, x_sample, cache_attn_a_k, cache_attn_a_v, cache_attn_b_k, cache_attn_b_v, state_ret,
              c, c_ctx, w_ada, b_ada, w_in, w_out, attn_sink, na_rpb, ret_decay, ret_gn,
              ln1_g, ln1_b, ln2_g, ln2_b, w_router, w_gate_up, w_down):
    alpha = (2 * DEPTH) ** 0.25

    x = x_prompt
    a_k, a_v, b_k, b_v, st = [], [], [], [], []
    for l in range(DEPTH):
        sh1, sc1, g1, sh2, sc2, g2 = _modulation(c_ctx[None, :], w_ada[l], b_ada[l])
        lg = -jnp.exp(ret_decay[l].astype(F32))
        mix, (ka, va, kb, vb, s_l) = _mixer_context(x * (1 + sc1) + sh1, w_in[l], w_out[l], attn_sink[l], lg, ret_gn[l])
        x = _layernorm(alpha * x + g1 * mix, ln1_g[l], ln1_b[l])
        ff = _expert_choice(x * (1 + sc2) + sh2, w_router[l], w_gate_up[l], w_down[l])
        x = _layernorm(alpha * x + g2 * ff, ln2_g[l], ln2_b[l])
        a_k.append(ka); a_v.append(va); b_k.append(kb); b_v.append(vb); st.append(s_l)
    y_prompt = x
    new_attn_a_k = jnp.stack(a_k, 1)
    new_attn_a_v = jnp.stack(a_v, 1)
    new_attn_b_k = jnp.stack(b_k, 1)
    new_attn_b_v = jnp.stack(b_v, 1)
    new_state_ret = jnp.stack(st, 1)

    x = x_sample
    for l in range(DEPTH):
        sh1, sc1, g1, sh2, sc2, g2 = _modulation(c, w_ada[l], b_ada[l])
        lg = -jnp.exp(ret_decay[l].astype(F32))
        mix = _mixer_latent(x * (1 + sc1) + sh1, cache_attn_a_k[:, l], cache_attn_a_v[:, l],
                            cache_attn_b_k[:, l], cache_attn_b_v[:, l], state_ret[:, l],
                            w_in[l], w_out[l], attn_sink[l], na_rpb[l], lg, ret_gn[l])
        x = _layernorm(alpha * x + g1 * mix, ln1_g[l], ln1_b[l])
        ff = _expert_choice(x * (1 + sc2) + sh2, w_router[l], w_gate_up[l], w_down[l])
        x = _layernorm(alpha * x + g2 * ff, ln2_g[l], ln2_b[l])
    y_sample = x

    return (y_prompt, y_sample, new_attn_a_k, new_attn_a_v, new_attn_b_k, new_attn_b_v, new_state_ret)
```

```python
import contextlib
import sys
import numpy as np
import concourse.bass as bass
import concourse.mybir as mybir
from concourse.bass_utils import run_bass_kernel_spmd

F32 = mybir.dt.float32
BF16 = mybir.dt.bfloat16
I32 = mybir.dt.int32
AF = mybir.ActivationFunctionType
ALU = mybir.AluOpType
AX = mybir.AxisListType

ENGS = ("tensor", "vector", "scalar", "gpsimd", "sync")
NDMA_HW = 40
NDMA_SW = 8
NDMA_SEMS = NDMA_HW + NDMA_SW


class Op:
    __slots__ = ("eng", "fn", "reads", "writes", "is_dma", "waits", "inc", "dma_sem", "dma_val", "prewait", "swdma", "line")

    def __init__(self, eng, fn, reads, writes, is_dma):
        self.eng = eng
        self.fn = fn
        self.reads = reads
        self.writes = writes
        self.is_dma = is_dma
        self.waits = {}
        self.inc = False
        self.dma_sem = None
        self.dma_val = None
        self.prewait = []
        self.swdma = False


class Prog:
    def __init__(self):
        self.ops = []
        self.eng_ops = {e: [] for e in ENGS}
        self.last_w = {}
        self.readers = {}
        self.dma_count = 0
        self.swdma_count = 0
        self.dma_sem_uses = [0] * NDMA_SEMS
        self.dma_sem_last = [None] * NDMA_SEMS

    def op(self, eng, fn, reads=(), writes=(), dma=False):
        sw = dma and eng == "gpsimd"
        o = Op(eng, fn, tuple(reads), tuple(writes), dma)
        o.swdma = sw
        deps = []
        for k in o.reads:
            w = self.last_w.get(k)
            if w is not None:
                deps.append(w)
        for k in o.writes:
            w = self.last_w.get(k)
            if w is not None:
                deps.append(w)
            deps.extend(self.readers.get(k, ()))
        if dma:
            if sw:
                j = NDMA_HW + (self.swdma_count % NDMA_SW)
                self.swdma_count += 1
            else:
                j = self.dma_count % NDMA_HW
                self.dma_count += 1
            prev = self.dma_sem_last[j]
            if prev is not None:
                deps.append(prev)
            self.dma_sem_uses[j] += 1
            o.dma_sem = j
            o.dma_val = 16 * self.dma_sem_uses[j]
            self.dma_sem_last[j] = o
        o.prewait = deps
        o.line = sys._getframe(2).f_lineno
        for k in o.reads:
            self.readers.setdefault(k, []).append(o)
        for k in o.writes:
            self.last_w[k] = o
            self.readers[k] = []
        self.eng_ops[eng].append(o)
        self.ops.append(o)
        return o

    def barrier(self):
        lasts = [self.eng_ops[e][-1] for e in ENGS if self.eng_ops[e]]
        lasts += [d for d in self.dma_sem_last if d is not None]
        for e in ENGS:
            o = Op(e, None, (), (), False)
            o.prewait = list(lasts)
            self.eng_ops[e].append(o)
            self.ops.append(o)

    @staticmethod
    def _needs_wait(d, o):
        if d.is_dma or o.is_dma:
            return True
        if d.eng != o.eng:
            return True
        return d.eng != "tensor"

    def finalize(self):
        for o in self.ops:
            for d in o.prewait:
                if d.fn is None:
                    continue
                if not d.is_dma and self._needs_wait(d, o):
                    d.inc = True
        self.inc_val = {}
        for e in ENGS:
            c = 0
            for o in self.eng_ops[e]:
                if o.inc and not o.is_dma:
                    c += 1
                    self.inc_val[id(o)] = c
        for o in self.ops:
            need = {}
            for d in o.prewait:
                if d.fn is None:
                    continue
                if d.is_dma:
                    key = ("dma", d.dma_sem)
                    v = d.dma_val
                else:
                    if not self._needs_wait(d, o):
                        continue
                    key = ("eng", d.eng)
                    v = self.inc_val[id(d)]
                if need.get(key, 0) < v:
                    need[key] = v
            o.waits = need
        for e in ENGS:
            seen = {}
            for o in self.eng_ops[e]:
                w2 = {}
                for k, v in o.waits.items():
                    if seen.get(k, 0) >= v:
                        continue
                    seen[k] = v
                    w2[k] = v
                o.waits = w2

    def emit(self, nc, max_ops=None):
        if max_ops is not None:
            keep = set(id(o) for o in self.ops[:max_ops])
            self.ops = self.ops[:max_ops]
            for e in ENGS:
                self.eng_ops[e] = [o for o in self.eng_ops[e] if id(o) in keep]
        self.finalize()
        with contextlib.ExitStack() as st:
            sems = {}
            for e in ENGS:
                sems[("eng", e)] = st.enter_context(nc.semaphore("s_" + e))
            for j in range(NDMA_SEMS):
                sems[("dma", j)] = st.enter_context(nc.semaphore("s_dma%d" % j))
            block = st.enter_context(nc.Block())
            prog = self

            def run(e, engobj):
                for o in prog.eng_ops[e]:
                    for k, v in o.waits.items():
                        engobj.wait_ge(sems[k], v)
                    if o.fn is None:
                        continue
                    ins = o.fn(engobj)
                    if o.is_dma:
                        ins.then_inc(sems[("dma", o.dma_sem)], 16)
                    elif o.inc:
                        ins.then_inc(sems[("eng", e)], 1)
                last = {}
                for o in prog.eng_ops[e]:
                    if o.is_dma:
                        last[o.dma_sem] = max(last.get(o.dma_sem, 0), o.dma_val)
                for j, v in last.items():
                    engobj.wait_ge(sems[("dma", j)], v)

            @block.sync
            def _(eng):
                run("sync", eng)

            @block.tensor
            def _(eng):
                run("tensor", eng)

            @block.vector
            def _(eng):
                run("vector", eng)

            @block.scalar
            def _(eng):
                run("scalar", eng)

            @block.gpsimd
            def _(eng):
                run("gpsimd", eng)


D = 1024
DEPTH = 2
NCTX = 256
NL = 2048
PAST = 512
GRID_W = 64
E = 16
DFF = 1024
NCOL = 2688
ALPHA = float((2 * DEPTH) ** 0.25)
EPS = 1e-6
NEG = -30000.0
NTOK = 512 + NL
NGB = NTOK // 128
CAP_C = 32
CAP_L = 256
NSLOT = 384

C_ID = 0
C_TRI1 = 128
C_TRI2 = 256
C_SU = 384
C_ONE = 512
C_RELP = 640
C_MF = 768
C_RELN = 896
C_MB = 1024
C_XIF = 1152
C_XIB = 1280
C_IOTA = 1408
C_ZF = 1664
C_ZB = 1665
C_PLO = 1666
C_PADS = 1667
NCF = 1668


def _consts():
    c = np.zeros((128, NCF), np.float32)
    p = np.arange(128)[:, None].astype(np.float32)
    i = np.arange(128)[None, :].astype(np.float32)
    c[:, C_ID:C_ID + 128] = (p == i)
    c[:, C_TRI1:C_TRI1 + 128] = (i <= p)
    c[:, C_TRI2:C_TRI2 + 128] = (p <= i)
    c[:, C_SU:C_SU + 128] = (p < i)
    c[:, C_ONE:C_ONE + 128] = 1.0
    c[:, C_RELP:C_RELP + 128] = np.maximum(i - p, 0)
    c[:, C_MF:C_MF + 128] = 0.125 * (i >= p)
    c[:, C_RELN:C_RELN + 128] = np.maximum(p - i, 0)
    c[:, C_MB:C_MB + 128] = 0.125 * (p >= i)
    c[:, C_XIF:C_XIF + 128] = i + 1
    c[:, C_XIB:C_XIB + 128] = 128 - i
    c[:, C_IOTA:C_IOTA + 256] = np.arange(256)[None, :]
    c[:, C_ZF] = 127 - p[:, 0]
    c[:, C_ZB] = p[:, 0]
    c[:, C_PLO] = p[:, 0]
    c[:, C_PADS] = 2560 + p[:, 0]
    return c


def _rope_tables():
    t = np.arange(NL)
    row, col = t // GRID_W, t % GRID_W
    inv = 1.0 / (10000.0 ** (np.arange(0, 32, 2) / 32.0))
    ar = (row[:, None] * inv[None, :]).astype(np.float32)
    ac = (col[:, None] * inv[None, :]).astype(np.float32)
    cos = np.concatenate([np.cos(ar), np.cos(ar), np.cos(ac), np.cos(ac)], -1).astype(np.float32)
    sin = np.concatenate([-np.sin(ar), np.sin(ar), -np.sin(ac), np.sin(ac)], -1).astype(np.float32)
    return cos, sin


def _aug_tables():
    tok = np.arange(NL)
    r = tok // GRID_W
    kr = np.arange(32)[:, None]
    augk = (r[None, :] == kr).astype(np.float32)
    rs = np.clip(r - 4, 0, 24)
    valid = (kr >= rs[None, :]) & (kr < rs[None, :] + 8)
    augq = np.where(valid, 0.0, NEG).astype(np.float32)
    return augk, augq


def _expand_rpb(rpb):
    a = (np.arange(128) // 64)[:, None, None]
    kc = (np.arange(128) % 64)[:, None, None]
    m = (np.arange(22) - 3)[None, :, None]
    c = np.arange(64)[None, None, :]
    drp = m - a
    dr = 14 - drp
    dc = kc - c + 15
    cs = np.clip(c - 8, 0, 48)
    valid = (drp >= 0) & (drp <= 14) & (kc >= cs) & (kc < cs + 16)
    dr_c = np.clip(dr, 0, 14)
    dc_c = np.clip(dc, 0, 30)
    dr_b, dc_b, valid_b = np.broadcast_arrays(dr_c, dc_c, valid)
    out = np.where(valid_b[None, None], rpb[:, :, dr_b, dc_b], np.float32(NEG)).astype(np.float32)
    return out.reshape(rpb.shape[0], 4, 128, 22 * 64)


ARENA_W = 53000


class Arena:
    def __init__(self, ap):
        self.ap = ap
        self.top = 0

    def f32(self, n):
        off = self.top
        self.top += n
        assert self.top <= ARENA_W, ("SBUF arena overflow", self.top)
        return self.ap[:, off:off + n]

    def bf(self, nel):
        return self.f32((nel + 1) // 2).bitcast(BF16)

    def i32(self, n):
        return self.f32(n).bitcast(I32)


def build_nc(n_layers=DEPTH, do_lat=True, max_ops=None):
    nc = bass.Bass("TRN2", target_bir_lowering=False)

    def din(name, shape, dt=F32):
        return nc.dram_tensor(name, shape, dt, kind="ExternalInput").ap()

    def dout(name, shape):
        return nc.dram_tensor(name, shape, F32, kind="ExternalOutput").ap()

    xc = din("xc", [512, D])
    xl = din("xl", [NL, D])
    cond = din("cond", [2, D])
    w_ada = din("w_ada", [2, D, 6 * D])
    b_ada = din("b_ada", [2, 6 * D])
    w_in = din("w_in", [2, D, NCOL])
    w_out = din("w_out", [2, D, D])
    w_r = din("w_r", [2, D, E])
    w_gu = din("w_gu", [2, E, D, 2 * DFF])
    w_dn = din("w_dn", [2, E, DFF, D])
    sink = din("sink", [2, 8])
    tbd = din("tb", [2, 4, 128, 22 * 64])
    dec = din("dec", [2, 8])
    gn = din("gn", [2, 256])
    lnp_d = din("lnp", [2, 4, D])
    cak = din("cak", [2, PAST, 128])
    cav = din("cav", [2, PAST, 128])
    cbk = din("cbk", [2, PAST, 256])
    cbv = din("cbv", [2, PAST, 256])
    st0 = din("st0", [2, 2, 4, 64, 64])
    cfd = din("cf", [128, NCF])
    augk = din("augk", [32, NL])
    augq = din("augq", [32, NL])
    ropec = din("ropec", [NL, 64])
    ropes = din("ropes", [NL, 64])
    yc = dout("yc", [512, D])
    yl = dout("yl", [NL, D])
    oak = dout("oak", [2, 2, NCTX, 128])
    oav = dout("oav", [2, 2, NCTX, 128])
    obk = dout("obk", [2, 2, NCTX, 256])
    obv = dout("obv", [2, 2, NCTX, 256])
    ost = dout("ost", [2, 2, 2, 4, 64, 64])
    xls = nc.dram_tensor("xls", [NL, D], F32, kind="Internal").ap()
    catd = nc.dram_tensor("catd", [NL, D], BF16, kind="Internal").ap()
    htd = nc.dram_tensor("htd", [16, 128, D], BF16, kind="Internal").ap()
    hbuf = nc.dram_tensor("hbuf", [NTOK, D], BF16, kind="Internal").ap()
    ysc = nc.dram_tensor("ysc", [NTOK + 128, D], F32, kind="Internal").ap()

    P = Prog()
    stack = contextlib.ExitStack()
    arena_t = stack.enter_context(nc.sbuf_tensor("arena", [128, ARENA_W], F32))
    AR = Arena(arena_t)
    ps = [stack.enter_context(nc.psum_tensor("ps%d" % i, [128, 512], F32)) for i in range(8)]
    psb = [p[:].bitcast(BF16) for p in ps]

    def DMA(eng, out, in_, r=(), w=(), **kw):
        P.op(eng, lambda e: e.dma_start(out=out, in_=in_, **kw), r, w, dma=True)

    def MM(out, lhsT, rhs, st, sp, r, w):
        P.op("tensor", lambda e: e.matmul(out, lhsT=lhsT, rhs=rhs, start=st, stop=sp), r, w)

    def TR(out, in_, ident, r, w):
        P.op("tensor", lambda e: e.transpose(out, in_, ident), r, w)

    def TT(eng, out, in0, in1, op, r, w):
        P.op(eng, lambda e: e.tensor_tensor(out=out, in0=in0, in1=in1, op=op), r, w)

    def TS(eng, out, in0, s1, s2, op0, op1, r, w):
        if op1 is None:
            P.op(eng, lambda e: e.tensor_scalar(out=out, in0=in0, scalar1=s1, scalar2=None, op0=op0), r, w)
        else:
            P.op(eng, lambda e: e.tensor_scalar(out=out, in0=in0, scalar1=s1, scalar2=s2, op0=op0, op1=op1), r, w)

    def STT(eng, out, in0, sc, in1, op0, op1, r, w):
        P.op(eng, lambda e: e.scalar_tensor_tensor(out=out, in0=in0, scalar=sc, in1=in1, op0=op0, op1=op1), r, w)

    def CP(eng, out, in_, r, w):
        if eng == "scalar":
            P.op(eng, lambda e: e.activation(out=out, in_=in_, func=AF.Identity), r, w)
        else:
            P.op(eng, lambda e: e.tensor_copy(out=out, in_=in_), r, w)

    def ACT(out, in_, func, r, w, bias=None, scale=None, accum=None):
        kw = {}
        if bias is not None:
            kw["bias"] = bias
        if scale is not None:
            kw["scale"] = scale
        if accum is not None:
            kw["accum_out"] = accum
        P.op("scalar", lambda e: e.activation(out=out, in_=in_, func=func, **kw), r, w)

    def RED(eng, out, in_, op, r, w):
        P.op(eng, lambda e: e.tensor_reduce(out=out, in_=in_, axis=AX.X, op=op), r, w)

    def MS(eng, ap, val, w):
        P.op(eng, lambda e: e.memset(ap, val), (), w)

    def pk(i):
        return "ps%d" % i

    rot = {"mm": [0, 1], "tr": [2, 3], "sc": [4, 5], "sc4": [4, 5, 0, 1]}
    rotc = {"mm": 0, "tr": 0, "sc": 0, "sc4": 0}

    def bank(kind):
        i = rot[kind][rotc[kind] % len(rot[kind])]
        rotc[kind] += 1
        return i

    CF = AR.f32(NCF)
    CB = AR.bf(640)
    XC = AR.f32(4 * D).rearrange("p (b d) -> p b d", b=4)
    MREG = AR.f32(5 * D)
    LNG = MREG[:, 0:D]
    LNB = MREG[:, D:2 * D]
    M0 = MREG[:, 2 * D:3 * D]
    M1 = MREG[:, 3 * D:4 * D]
    M2 = MREG[:, 4 * D:5 * D]
    SCB = AR.bf(2 * 8 * 128).rearrange("p (c k m) -> p c k m", c=2, k=8)
    SM = AR.f32(160)
    CT = SM[:, 0:16].rearrange("p (c k) -> p c k", c=2)
    ESINK = SM[:, 16:24]
    LGT = SM[:, 24:32]
    EPSB = SM[:, 32:33]
    ZEROB = SM[:, 33:34]
    ONEB = SM[:, 34:35]
    GW = AR.f32(256)
    WRB = AR.bf(8 * 16).rearrange("p (k e) -> p k e", k=8)
    WRF = AR.f32(8 * 16).rearrange("p (k e) -> p k e", k=8)
    AFF = AR.f32(NGB * 16).rearrange("p (b e) -> p b e", b=NGB)
    WST = [AR.f32(8 * 256).rearrange("p (k c) -> p k c", k=8) for _ in range(2)]
    PERSIST_TOP = AR.top
    WSL = []

    def alloc_wsl(n):
        del WSL[:]
        for _ in range(n):
            WSL.append(AR.bf(8 * 512).rearrange("p (k c) -> p k c", k=8))

    IDB = CB[:, C_ID:C_ID + 128]
    IDF = CF[:, C_ID:C_ID + 128]
    TRI1 = CB[:, C_TRI1:C_TRI1 + 128]
    TRI2 = CB[:, C_TRI2:C_TRI2 + 128]
    SUB = CB[:, C_SU:C_SU + 128]
    ONESB = CB[:, C_ONE:C_ONE + 128]

    wctr = [0, 0]

    def wload(src3, cast_eng="scalar"):
        ncols = src3.shape[2]
        j = wctr[1] % len(WSL)
        wctr[1] += 1
        key = "wsl%d" % j
        for kh in range(2):
            s_ = wctr[0] % len(WST)
            wctr[0] += 1
            stv = WST[s_].rearrange("p k c -> p (k c)")[:, 0:4 * ncols].rearrange("p (k c) -> p k c", k=4)
            DMA("sync", stv, src3[:, kh * 4:(kh + 1) * 4, :], w=["wst%d" % s_])
            CP(cast_eng, WSL[j][:, kh * 4:(kh + 1) * 4, 0:ncols], stv, r=["wst%d" % s_], w=[key + "_%d" % kh])
        return WSL[j][:, :, 0:ncols], [key + "_0", key + "_1"]

    def wsrc(w2d, c0, c1):
        return w2d.rearrange("(k p) c -> p k c", p=128)[:, :, c0:c1]

    DMA("sync", CF, cfd, w=["CF"])
    CP("vector", CB, CF[:, 0:640], r=["CF"], w=["CB"])
    DMA("sync", XC, xc.rearrange("(b p) d -> p b d", p=128), w=["XC0", "XC1", "XC2", "XC3"])
    DMA("sync", CT, cond.rearrange("c (k p) -> p c k", p=128), w=["CT"], allow_slow_non_contiguous=True)
    MS("vector", EPSB, EPS, ["EPSB"])
    MS("vector", ZEROB, 0.0, ["ZEROB"])
    MS("vector", ONEB, 1.0, ["ONEB"])
    ACT(CT, CT, AF.Silu, r=["CT"], w=["CT"])
    for c in range(2):
        CP("vector", SCB[:, c], CT[:, c, :].unsqueeze(2).to_broadcast([128, 8, 128]), r=["CT"], w=["SCB"])

    def get_mod(dst, dkey, c, l, part, add_one):
        DMA("sync", dst, b_ada[l, part * D:(part + 1) * D].partition_broadcast(128), w=[dkey])
        for ct in range(2):
            sl, sk = wload(wsrc(w_ada[l], part * D + ct * 512, part * D + ct * 512 + 512))
            b = bank("mm")
            for k in range(8):
                MM(ps[b][:, :], SCB[:, c, k, :], sl[:, k, :], k == 0, k == 7, r=sk + ["SCB"], w=[pk(b)])
            STT("vector", dst[:, ct * 512:(ct + 1) * 512], ps[b][:, :], 1.0 if add_one else 0.0,
                dst[:, ct * 512:(ct + 1) * 512], ALU.add, ALU.add, r=[pk(b)], w=[pk(b), dkey])

    class Stream:
        pass

    SC_ = Stream()
    SC_.lat = False
    SC_.nblk = 4
    SC_.cond = 0
    SC_.gb0 = 0
    SL_ = Stream()
    SL_.lat = True
    SL_.nblk = 16
    SL_.cond = 1
    SL_.gb0 = 4

    def ln_epilogue(T1, xin_ap, xin_keys, gmod, gkey, out_ap, out_keys, tmp, tkey, mv_tiles):
        stats, mv, rstd, nmr = mv_tiles
        TT("vector", T1, T1, gmod, ALU.mult, r=[tkey, gkey], w=[tkey])
        STT("vector", T1, xin_ap, ALPHA, T1, ALU.mult, ALU.add, r=[tkey] + xin_keys, w=[tkey])
        P.op("vector", lambda e: e.bn_stats(out=stats[:, 0:6], in_=T1[:, 0:512]), [tkey], ["lnst"])
        P.op("vector", lambda e: e.bn_stats(out=stats[:, 6:12], in_=T1[:, 512:1024]), [tkey], ["lnst2"])
        P.op("vector", lambda e: e.bn_aggr(out=mv, in_=stats), ["lnst", "lnst2"], ["lnmv"])
        ACT(rstd, mv[:, 1:2], AF.Ln, r=["lnmv", "EPSB"], w=["lnrs"], bias=EPSB)
        ACT(rstd, rstd, AF.Exp, r=["lnrs"], w=["lnrs"], scale=-0.5)
        STT("vector", nmr, mv[:, 0:1], -1.0, rstd, ALU.mult, ALU.mult, r=["lnmv", "lnrs"], w=["lnnm"])
        ACT(tmp, T1, AF.Identity, r=[tkey, "lnrs", "lnnm"], w=[tkey + "n"], bias=nmr, scale=rstd)
        TT("gpsimd", tmp, tmp, LNG, ALU.mult, r=[tkey + "n", "LNG"], w=[tkey + "n"])
        TT("gpsimd", out_ap, tmp, LNB, ALU.add, r=[tkey + "n", "LNB"], w=out_keys)


    hctr = [0]
    octr = [0]

    def obank():
        i = 6 + (octr[0] % 2)
        octr[0] += 1
        return i

    def mixer(S, l):
        nblk = S.nblk
        lat = S.lat
        NT = nblk * 128
        nreq = 1 if lat else 2
        bpr = nblk // nreq
        xsrc = (xl if l == 0 else xls) if lat else None
        alloc_wsl(3)
        del WST[2:]
        for _ in range(1 if lat else 2):
            WST.append(AR.f32(8 * 256).rearrange("p (k c) -> p k c", k=8))
        get_mod(M0, "M0", S.cond, l, 1, True)
        get_mod(M1, "M1", S.cond, l, 0, False)
        XS = [AR.f32(D), AR.f32(D)]
        HF = AR.f32(D)
        HB = [AR.bf(D), AR.bf(D)]
        HT = [AR.bf(8 * 128).rearrange("p (k t) -> p k t", k=8) for _ in range(2)]
        STG = [AR.bf(512), AR.bf(512)]
        CS = [AR.bf(4 * 512).rearrange("p (s c) -> p s c", s=4) for _ in range(2)]
        PTH = [None, None]
        SMX = AR.f32(64)
        DEN = SMX[:, 0:4]
        RDEN = SMX[:, 4:8]
        KO = [AR.f32(256), AR.f32(256)] if not lat else None
        sctr = [0]
        cctr = [0]
        kctr = [0]
        GTOP = AR.top

        def xblock(b):
            if lat:
                i = hctr[0] % 2
                hctr[0] += 1
                DMA("sync", XS[i], xsrc[b * 128:(b + 1) * 128, :], r=["xls"], w=["XS%d" % i])
                return XS[i], ["XS%d" % i]
            return XC[:, b, :], ["XC%d" % b]

        def to_hT(hb, hbk):
            i = cctr[0] % 2
            cctr[0] += 1
            tb_ = bank("tr")
            for k in range(8):
                TR(psb[tb_][:, k * 128:(k + 1) * 128], hb[:, k * 128:(k + 1) * 128], IDB, r=[hbk, "CB"], w=[pk(tb_)])
            CP("scalar", HT[i], psb[tb_][:, :].rearrange("p (k t) -> p k t", k=8), r=[pk(tb_)], w=[pk(tb_), "HT%d" % i])
            return HT[i], "HT%d" % i

        hmode = ["compute"]

        def hblock(b):
            if hmode[0] == "load":
                i = cctr[0] % 2
                cctr[0] += 1
                DMA("sync", HT[i].rearrange("p k t -> p (k t)"), htd[b], r=["htd"], w=["HT%d" % i])
                return HT[i], "HT%d" % i
            xap, xk = xblock(b)
            i = kctr[0] % 2
            kctr[0] += 1
            TT("vector", HF, xap, M0, ALU.mult, r=xk + ["M0"], w=["HF"])
            TT("vector", HB[i], HF, M1, ALU.add, r=["HF", "M1"], w=["HB%d" % i])
            ht, htk = to_hT(HB[i], "HB%d" % i)
            DMA("gpsimd", htd[b], ht.rearrange("p k t -> p (k t)"), r=[htk], w=["htd"])
            return ht, htk

        def proj(ht, htk, sl, sk, ncols):
            b = bank("mm")
            for k in range(8):
                MM(ps[b][:, 0:ncols], ht[:, k, :], sl[:, k, 0:ncols], k == 0, k == 7, r=[htk] + sk, w=[pk(b)])
            return b

        def stage():
            i = sctr[0] % 2
            sctr[0] += 1
            return STG[i], "STG%d" % i

        def attn_scores(job, par):
            qT_fn, qkeys, nsub, blocks, esink_ap, cs_dst, cs_key, post = job
            PT = PTH[par]
            for bi, blk in enumerate(blocks):
                sb_ = bank("sc4")
                w = (blk["hi"] - blk["lo"]) * 128
                ptk = "PT%d_%d" % (par, bi)
                MM(ps[sb_][:, 0:w], blk["kT"], qT_fn(blk["lo"], blk["hi"], blk["K"]), True, True,
                   r=blk["r"] + qkeys, w=[pk(sb_)])
                ACT(PT[:, bi, 0:w], ps[sb_][:, 0:w], AF.Exp, r=[pk(sb_)], w=[pk(sb_), ptk], scale=0.125)
                for (c0, c1, m, mk) in blk["masks"]:
                    TT("vector" if (c1 - c0) > 128 else "gpsimd", PT[:, bi, c0:c1], PT[:, bi, c0:c1], m, ALU.mult, r=[ptk] + mk, w=[ptk])

        def attn_pv(job, par):
            qT_fn, qkeys, nsub, blocks, esink_ap, cs_dst, cs_key, post = job
            PT = PTH[par]
            ob = obank()
            for s_ in range(nsub):
                rel = [(bi, blk) for bi, blk in enumerate(blocks) if blk["lo"] <= s_ < blk["hi"]]
                for n_, (bi, blk) in enumerate(rel):
                    o0 = (s_ - blk["lo"]) * 128
                    MM(ps[ob][:, s_ * 65:(s_ + 1) * 65], PT[:, bi, o0:o0 + 128], blk["va"], n_ == 0, n_ == len(rel) - 1,
                       r=["PT%d_%d" % (par, bi)] + blk["r"], w=[pk(ob)])
            O = ps[ob][:, 0:nsub * 65].rearrange("p (s d) -> p s d", d=65)
            TS("vector", DEN[:, 0:nsub], O[:, :, 64], esink_ap if esink_ap is not None else 0.0, None, ALU.add, None,
               r=[pk(ob), "ESINK"], w=[pk(ob), "DEN"])
            P.op("vector", lambda e: e.reciprocal(out=RDEN[:, 0:nsub], in_=DEN[:, 0:nsub]), ["DEN"], ["RDEN"])
            TT("vector", cs_dst, O[:, :, 0:64], RDEN[:, 0:nsub].unsqueeze(2).to_broadcast([128, nsub, 64]), ALU.mult,
               r=[pk(ob), "RDEN"], w=[pk(ob), cs_key])
            if post is not None:
                post()

        def run_attn(jobs, mid_hook=None):
            if not jobs:
                return
            attn_scores(jobs[0], 0)
            for i, job in enumerate(jobs):
                if i + 1 < len(jobs):
                    attn_scores(jobs[i + 1], (i + 1) % 2)
                attn_pv(job, i % 2)
                if mid_hook is not None and i == len(jobs) // 3:
                    mid_hook()

        def cs_to_catT(cs, cs_key, sub0, nsub, nch, gc0):
            dst = catd[sub0 * 128:(sub0 + nsub) * 128, gc0 * 128:(gc0 + nch) * 128].rearrange("(s p) c -> p s c", p=128)
            DMA("gpsimd", dst, cs[:, 0:nsub, 0:nch * 128], r=[cs_key], w=["catd"])

        def rope(src, nh, b, dst, dkey, srckey, RC, RS, R1, R2):
            x3 = src.rearrange("p (h d) -> p h d", h=nh)
            x5 = src.rearrange("p (h r f x) -> p h r f x", h=nh, r=2, f=2)
            r5 = R2[:, 0:nh * 64].rearrange("p (h r f x) -> p h r f x", h=nh, r=2, f=2)
            c3 = RC[:, b, :].unsqueeze(1).to_broadcast([128, nh, 64])
            s4 = RS[:, b, :].rearrange("p (r f x) -> p r f x", r=2, f=2)
            TT("vector", R1[:, 0:nh * 64].rearrange("p (h d) -> p h d", h=nh), x3, c3, ALU.mult, r=[srckey, "ROPE"], w=[srckey, "R1"])
            TT("vector", r5[:, :, :, 0, :], x5[:, :, :, 1, :], s4[:, :, 0, :].unsqueeze(1).to_broadcast([128, nh, 2, 16]), ALU.mult,
               r=[srckey, "ROPE"], w=[srckey, "R2a"])
            TT("vector", r5[:, :, :, 1, :], x5[:, :, :, 0, :], s4[:, :, 1, :].unsqueeze(1).to_broadcast([128, nh, 2, 16]), ALU.mult,
               r=[srckey, "ROPE"], w=[srckey, "R2b"])
            TT("gpsimd", dst, R1[:, 0:nh * 64], R2[:, 0:nh * 64], ALU.add, r=["R1", "R2a", "R2b"], w=[dkey])

        def group_A():
            QT = AR.bf(4 * NT).rearrange("p (q t) -> p q t", q=4)
            KT = AR.bf(2 * NT).rearrange("p (q t) -> p q t", q=2)
            VA = AR.bf(nblk * 2 * 65).rearrange("p (b h d) -> p b h d", b=nblk, h=2)
            PTH[0] = AR.bf(12 * 512).rearrange("p (b c) -> p b c", b=12)
            PTH[1] = AR.bf(12 * 512).rearrange("p (b c) -> p b c", b=12)
            MS("gpsimd", VA, 1.0, ["VA"])
            if lat:
                KCT = AR.bf(2 * 512).rearrange("p (q t) -> p q t", q=2)
                VCA = AR.bf(4 * 2 * 65).rearrange("p (b h d) -> p b h d", b=4, h=2)
                CST = AR.f32(4 * 128).rearrange("p (b c) -> p b c", b=4)
                CSB = AR.bf(4 * 256).rearrange("p (b c) -> p b c", b=4)
                RC = AR.f32(nblk * 64).rearrange("p (b d) -> p b d", b=nblk)
                RS = AR.f32(nblk * 64).rearrange("p (b d) -> p b d", b=nblk)
                R1 = AR.f32(512)
                R2 = AR.f32(512)
                DMA("sync", RC, ropec.rearrange("(b p) d -> p b d", p=128), w=["ROPE"])
                DMA("sync", RS, ropes.rearrange("(b p) d -> p b d", p=128), w=["ROPE"])
                MS("gpsimd", VCA, 1.0, ["VCA"])
                DMA("sync", CST, cak[l].rearrange("(b p) c -> p b c", p=128), w=["CST"])
                CP("vector", CSB.rearrange("p b (h u d) -> p b h u d", h=2, u=2)[:, :, :, 0, :],
                   CST.rearrange("p b (h d) -> p b h d", h=2), r=["CST"], w=["CSBa"])
                CP("vector", CSB.rearrange("p b (h u d) -> p b h u d", h=2, u=2)[:, :, :, 1, :],
                   CST.rearrange("p b (h d) -> p b h d", h=2), r=["CST"], w=["CSBb"])
                for kv in range(2):
                    tb_ = bank("tr")
                    for cb in range(4):
                        TR(psb[tb_][:, cb * 128:(cb + 1) * 128], CSB[:, cb, kv * 128:(kv + 1) * 128], IDB, r=["CSBa", "CSBb", "CB"], w=[pk(tb_)])
                    CP("vector", KCT[:, kv, :], psb[tb_][:, 0:512], r=[pk(tb_)], w=[pk(tb_), "KCT"])
                DMA("sync", CST, cav[l].rearrange("(b p) c -> p b c", p=128), r=["CSBa", "CSBb"], w=["CST"])
                CP("vector", VCA[:, :, :, 0:64], CST.rearrange("p b (h d) -> p b h d", h=2), r=["CST"], w=["VCA"])
            w0, k0 = wload(wsrc(w_in[l], 0, 512))
            w1, k1 = wload(wsrc(w_in[l], 512, 896))
            nxt = hblock(0)
            for b in range(nblk):
                ht, htk = nxt
                if b + 1 < nblk:
                    nxt = hblock(b + 1)
                b0 = proj(ht, htk, w0, k0, 512)
                sg, sgk = stage()
                if lat:
                    rope(ps[b0][:, 0:512], 8, b, sg, sgk, pk(b0), RC, RS, R1, R2)
                else:
                    CP("scalar", sg, ps[b0][:, 0:512], r=[pk(b0)], w=[pk(b0), sgk])
                tb_ = bank("tr")
                for q in range(4):
                    TR(psb[tb_][:, q * 128:(q + 1) * 128], sg[:, q * 128:(q + 1) * 128], IDB, r=[sgk, "CB"], w=[pk(tb_)])
                CP("vector", QT[:, :, b * 128:(b + 1) * 128], psb[tb_][:, 0:512].rearrange("p (q t) -> p q t", q=4),
                   r=[pk(tb_)], w=[pk(tb_), "QT"])
                b1 = proj(ht, htk, w1, k1, 384)
                sg, sgk = stage()
                if lat:
                    rope(ps[b1][:, 0:256], 4, b, sg[:, 0:256], sgk, pk(b1), RC, RS, R1, R2)
                else:
                    CP("scalar", sg[:, 0:256], ps[b1][:, 0:256], r=[pk(b1)], w=[pk(b1), sgk])
                tb_ = bank("tr")
                for q in range(2):
                    TR(psb[tb_][:, q * 128:(q + 1) * 128], sg[:, q * 128:(q + 1) * 128], IDB, r=[sgk, "CB"], w=[pk(tb_)])
                CP("vector", KT[:, :, b * 128:(b + 1) * 128], psb[tb_][:, 0:256].rearrange("p (q t) -> p q t", q=2),
                   r=[pk(tb_)], w=[pk(tb_), "KT"])
                CP("scalar", VA[:, b, :, 0:64], ps[b1][:, 256:384].rearrange("p (h d) -> p h d", h=2), r=[pk(b1)], w=[pk(b1), "VA"])
                if not lat:
                    rq, tb0 = divmod(b, 2)
                    ko = KO[b % 2]
                    kok = "KO%d" % (b % 2)
                    CP("vector", ko[:, 0:128].rearrange("p (h d) -> p h d", h=2),
                       ps[b1][:, 0:256].rearrange("p (h u d) -> p h u d", h=2, u=2)[:, :, 0, :], r=[pk(b1)], w=[pk(b1), kok])
                    CP("vector", ko[:, 128:256], ps[b1][:, 256:384], r=[pk(b1)], w=[pk(b1), kok])
                    DMA("gpsimd", oak[rq, l, tb0 * 128:(tb0 + 1) * 128, :], ko[:, 0:128], r=[kok])
                    DMA("gpsimd", oav[rq, l, tb0 * 128:(tb0 + 1) * 128, :], ko[:, 128:256], r=[kok])
            ntile = nblk // 4 if lat else nreq
            nsub = 4 if lat else 2
            jobs = []
            for t in range(ntile):
                cs = CS[t % 2]
                csk = "CS%d" % (t % 2)
                for h in range(8):
                    p_, a_ = divmod(h, 2)
                    kv = h // 4
                    rows = slice(a_ * 64, (a_ + 1) * 64)
                    blocks = []
                    if lat:
                        for cb in range(4):
                            blocks.append(dict(kT=KCT[rows, kv, cb * 128:(cb + 1) * 128], va=VCA[:, cb, kv, :], lo=0, hi=4,
                                               K=64, masks=[], r=["KCT", "VCA"]))
                        for j in range(max(0, 4 * t - 1), min(15, 4 * t + 4) + 1):
                            lo = max(j - 1, 4 * t) - 4 * t
                            hi = min(j + 1, 4 * t + 3) - 4 * t + 1
                            masks = []
                            for i in range(lo + 4 * t, hi + 4 * t):
                                c0 = (i - 4 * t - lo) * 128
                                if i == j + 1:
                                    masks.append((c0, c0 + 128, TRI1, ["CB"]))
                                elif i == j - 1:
                                    masks.append((c0, c0 + 128, TRI2, ["CB"]))
                            blocks.append(dict(kT=KT[rows, kv, j * 128:(j + 1) * 128], va=VA[:, j, kv, :], lo=lo, hi=hi,
                                               K=64, masks=masks, r=["KT", "VA"]))
                    else:
                        for j in range(2 * t, 2 * t + 2):
                            blocks.append(dict(kT=KT[rows, kv, j * 128:(j + 1) * 128], va=VA[:, j, kv, :], lo=0, hi=2,
                                               K=64, masks=[], r=["KT", "VA"]))
                    q0 = t * nsub

                    def qfn(lo, hi, K, p_=p_, rows=rows, q0=q0):
                        return QT[rows, p_, (q0 + lo) * 128:(q0 + hi) * 128]

                    post = None
                    if h == 7:
                        post = (lambda cs=cs, csk=csk, t=t: cs_to_catT(cs, csk, t * nsub, nsub, 4, 0))
                    jobs.append((qfn, ["QT"], nsub, blocks, ESINK[:, h:h + 1], cs[:, 0:nsub, h * 64:(h + 1) * 64], csk, post))

            def later_mods():
                get_mod(M2, "M2", S.cond, l, 2, False)
                get_mod(M0, "M0", S.cond, l, 4, True)
                get_mod(M1, "M1", S.cond, l, 3, False)

            run_attn(jobs, later_mods)

        def group_B(hp):
            QTB = AR.bf(2 * NT).rearrange("p (q t) -> p q t", q=2)
            KTB = AR.bf(2 * NT).rearrange("p (q t) -> p q t", q=2)
            VB = AR.bf(nblk * 2 * 65).rearrange("p (b h d) -> p b h d", b=nblk, h=2)
            PTH[0] = AR.bf(12 * 512).rearrange("p (b c) -> p b c", b=12)
            PTH[1] = AR.bf(12 * 512).rearrange("p (b c) -> p b c", b=12)
            MS("gpsimd", VB, 1.0, ["VB"])
            if lat:
                T2 = AR.bf(2 * 22 * 64).rearrange("p (h c) -> p h c", h=2)
                T2F = AR.f32(22 * 64)
                KCTB = AR.bf(2 * 512).rearrange("p (q t) -> p q t", q=2)
                VCB = AR.bf(4 * 2 * 65).rearrange("p (b h d) -> p b h d", b=4, h=2)
                CST = AR.f32(4 * 128).rearrange("p (b c) -> p b c", b=4)
                CSB = AR.bf(4 * 128).rearrange("p (b c) -> p b c", b=4)
                MS("gpsimd", VCB, 1.0, ["VCB"])
                for hh in range(2):
                    DMA("sync", T2F, tbd[l, 2 * hp + hh], w=["T2F"])
                    ACT(T2[:, hh, :], T2F, AF.Exp, r=["T2F"], w=["T2"])
                    for half in range(2):
                        cs_ = slice(half * 1024, (half + 1) * 1024)
                        DMA("gpsimd", KTB[64:96, hh, cs_], augk[:, cs_], w=["KTBaug"])
                        DMA("gpsimd", QTB[64:96, hh, cs_], augq[:, cs_], w=["QTBaug"])
                DMA("sync", CST, cbk[l].rearrange("(b p) c -> p b c", p=128)[:, :, hp * 128:(hp + 1) * 128], w=["CST"])
                CP("vector", CSB, CST, r=["CST"], w=["CSB"])
                for hh in range(2):
                    tb_ = bank("tr")
                    for cb in range(4):
                        TR(psb[tb_][0:64, cb * 128:(cb + 1) * 128], CSB[:, cb, hh * 64:(hh + 1) * 64], IDB, r=["CSB", "CB"], w=[pk(tb_)])
                    CP("vector", KCTB[0:64, hh, :], psb[tb_][0:64, 0:512], r=[pk(tb_)], w=[pk(tb_), "KCTB"])
                DMA("sync", CST, cbv[l].rearrange("(b p) c -> p b c", p=128)[:, :, hp * 128:(hp + 1) * 128], r=["CSB"], w=["CST"])
                CP("vector", VCB[:, :, :, 0:64], CST.rearrange("p b (h d) -> p b h d", h=2), r=["CST"], w=["VCB"])
            c0 = 896 + hp * 384
            w0, k0 = wload(wsrc(w_in[l], c0, c0 + 384))
            nxt = hblock(0)
            for b in range(nblk):
                ht, htk = nxt
                if b + 1 < nblk:
                    nxt = hblock(b + 1)
                b0 = proj(ht, htk, w0, k0, 384)
                sg, sgk = stage()
                CP("scalar", sg[:, 0:256], ps[b0][:, 0:256], r=[pk(b0)], w=[pk(b0), sgk])
                tb_ = bank("tr")
                for q in range(4):
                    TR(psb[tb_][0:64, q * 128:(q + 1) * 128], sg[:, q * 64:(q + 1) * 64], IDB, r=[sgk, "CB"], w=[pk(tb_)])
                CP("vector", QTB[0:64, :, b * 128:(b + 1) * 128], psb[tb_][0:64, 0:256].rearrange("p (q t) -> p q t", q=2),
                   r=[pk(tb_)], w=[pk(tb_), "QTB"])
                CP("vector", KTB[0:64, :, b * 128:(b + 1) * 128], psb[tb_][0:64, 256:512].rearrange("p (q t) -> p q t", q=2),
                   r=[pk(tb_)], w=[pk(tb_), "KTB"])
                CP("scalar", VB[:, b, :, 0:64], ps[b0][:, 256:384].rearrange("p (h d) -> p h d", h=2), r=[pk(b0)], w=[pk(b0), "VB"])
                if not lat:
                    rq, tb0 = divmod(b, 2)
                    ko = KO[b % 2]
                    kok = "KO%d" % (b % 2)
                    CP("vector", ko[:, 0:256], ps[b0][:, 128:384], r=[pk(b0)], w=[pk(b0), kok])
                    DMA("gpsimd", obk[rq, l, tb0 * 128:(tb0 + 1) * 128, hp * 128:(hp + 1) * 128], ko[:, 0:128], r=[kok])
                    DMA("gpsimd", obv[rq, l, tb0 * 128:(tb0 + 1) * 128, hp * 128:(hp + 1) * 128], ko[:, 128:256], r=[kok])
            ntile = nblk // 4 if lat else nreq
            nsub = 4 if lat else 2
            jobs = []
            for t in range(ntile):
                cs = CS[t % 2]
                csk = "CS%d" % (t % 2)
                for hh in range(2):
                    blocks = []
                    if lat:
                        for cb in range(4):
                            blocks.append(dict(kT=KCTB[0:64, hh, cb * 128:(cb + 1) * 128], va=VCB[:, cb, hh, :], lo=0, hi=4,
                                               K=64, masks=[], r=["KCTB", "VCB"]))
                        for j in range(max(0, 4 * t - 2), min(15, 4 * t + 5) + 1):
                            m0 = 2 * (4 * t - j) + 7
                            mk = T2[:, hh, (m0 + 3) * 64:(m0 + 3 + 8) * 64]
                            blocks.append(dict(kT=KTB[0:96, hh, j * 128:(j + 1) * 128], va=VB[:, j, hh, :], lo=0, hi=4,
                                               K=96, masks=[(0, 512, mk, ["T2"])], r=["KTB", "KTBaug", "VB"]))
                    else:
                        for j in range(2 * t, 2 * t + 2):
                            blocks.append(dict(kT=KTB[0:64, hh, j * 128:(j + 1) * 128], va=VB[:, j, hh, :], lo=0, hi=2,
                                               K=64, masks=[], r=["KTB", "VB"]))
                    q0 = t * nsub

                    def qfn(lo, hi, K, hh=hh, q0=q0):
                        return QTB[0:K, hh, (q0 + lo) * 128:(q0 + hi) * 128]

                    post = None
                    if hh == 1:
                        post = (lambda cs=cs, csk=csk, t=t: cs_to_catT(cs, csk, t * nsub, nsub, 1, 4 + hp))
                    jobs.append((qfn, ["QTB", "QTBaug"], nsub, blocks, None, cs[:, 0:nsub, hh * 64:(hh + 1) * 64], csk, post))
            run_attn(jobs)

        def group_C():
            QTC = AR.bf(2 * NT).rearrange("p (q t) -> p q t", q=2)
            KTC = AR.bf(2 * NT).rearrange("p (q t) -> p q t", q=2)
            KTOK = AR.bf(nblk * 256).rearrange("p (b c) -> p b c", b=nblk)
            VTOK = AR.bf(nblk * 256).rearrange("p (b c) -> p b c", b=nblk)
            GS = AR.bf(nblk * 256).rearrange("p (b c) -> p b c", b=nblk)
            DTT = AR.bf(8 * 128).rearrange("p (h i) -> p h i", h=8)
            DTF = AR.f32(128)
            XIP = AR.f32(4 * 128).rearrange("p (d q i) -> p d q i", d=2, q=2)
            ZT = AR.f32(8)
            GCHF = AR.f32(8)
            GCHP = AR.f32(4).rearrange("p (d q) -> p d q", d=2)
            SACC = AR.f32(4 * 128).rearrange("p (d q e) -> p d q e", d=2, q=2)
            SBF = AR.bf(2 * bpr * 2 * 64).rearrange("p (d c q e) -> p d c q e", d=2, c=bpr, q=2)
            KZ = [AR.bf(256), AR.bf(256)]
            PD = [AR.bf(8 * 128).rearrange("p (d h i) -> p d h i", d=2, h=4) for _ in range(2)]
            QX = [AR.bf(4 * 128).rearrange("p (d q i) -> p d q i", d=2, q=2) for _ in range(2)]
            O32 = AR.f32(256)
            SQ = AR.f32(256)
            GST = AR.f32(32)
            DMA("sync", LGT, dec[l, :].partition_broadcast(128), w=["LGT"])
            ACT(LGT, LGT, AF.Exp, r=["LGT"], w=["LGT"])
            TS("vector", LGT, LGT, -1.0, None, ALU.mult, None, r=["LGT"], w=["LGT"])
            for dh in range(8):
                d_ = dh // 4
                ACT(DTF, CF[:, (C_RELP if d_ == 0 else C_RELN):(C_RELP if d_ == 0 else C_RELN) + 128], AF.Exp,
                    r=["CF", "LGT"], w=["DTF"], scale=LGT[:, dh:dh + 1])
                TT("vector", DTT[:, dh, :], DTF, CF[:, (C_MF if d_ == 0 else C_MB):(C_MF if d_ == 0 else C_MB) + 128], ALU.mult,
                   r=["DTF", "CF"], w=["DTT"])
                zc = C_ZF if d_ == 0 else C_ZB
                ACT(ZT[:, dh:dh + 1], CF[:, zc:zc + 1], AF.Exp, r=["CF", "LGT"], w=["ZT"], scale=LGT[:, dh:dh + 1])
                ACT(GCHF[:, dh:dh + 1], LGT[:, dh:dh + 1], AF.Exp, r=["LGT"], w=["GCHF"], scale=128.0)
            TS("vector", ZT, ZT, 0.125, None, ALU.mult, None, r=["ZT"], w=["ZT"])
            for d_ in range(2):
                for q in range(2):
                    for a_ in range(2):
                        rows = slice(a_ * 64, (a_ + 1) * 64)
                        dh = d_ * 4 + 2 * q + a_
                        xc0 = C_XIF if d_ == 0 else C_XIB
                        ACT(XIP[rows, d_, q, :], CF[rows, xc0:xc0 + 128], AF.Exp, r=["CF", "LGT"], w=["XIP"], scale=LGT[rows, dh:dh + 1])
                        CP("vector", GCHP[rows, d_, q:q + 1], GCHF[rows, dh:dh + 1], r=["GCHF"], w=["GCHP"])
            w0, k0 = wload(wsrc(w_in[l], 1664, 2176))
            w1, k1 = wload(wsrc(w_in[l], 2176, 2688))
            nxt = hblock(0)
            for b in range(nblk):
                ht, htk = nxt
                if b + 1 < nblk:
                    nxt = hblock(b + 1)
                b0 = proj(ht, htk, w0, k0, 512)
                sg, sgk = stage()
                CP("scalar", sg, ps[b0][:, 0:512], r=[pk(b0)], w=[pk(b0), sgk])
                tb_ = bank("tr")
                for q in range(4):
                    TR(psb[tb_][:, q * 128:(q + 1) * 128], sg[:, q * 128:(q + 1) * 128], IDB, r=[sgk, "CB"], w=[pk(tb_)])
                CP("vector", QTC[:, :, b * 128:(b + 1) * 128], psb[tb_][:, 0:256].rearrange("p (q t) -> p q t", q=2),
                   r=[pk(tb_)], w=[pk(tb_), "QTC"])
                CP("vector", KTC[:, :, b * 128:(b + 1) * 128], psb[tb_][:, 256:512].rearrange("p (q t) -> p q t", q=2),
                   r=[pk(tb_)], w=[pk(tb_), "KTC"])
                CP("gpsimd", KTOK[:, b, :], sg[:, 256:512], r=[sgk], w=["KTOK"])
                b1 = proj(ht, htk, w1, k1, 512)
                CP("vector", VTOK[:, b, :], ps[b1][:, 0:256], r=[pk(b1)], w=[pk(b1), "VTOK"])
                ACT(GS[:, b, :], ps[b1][:, 256:512], AF.Silu, r=[pk(b1)], w=[pk(b1), "GS"])
            for rq in range(nreq):
                gb0 = rq * bpr
                MS("vector", SACC, 0.0, ["SACC0", "SACC1"])
                if lat:
                    for d_ in range(2):
                        for h in range(4):
                            q, a_ = divmod(h, 2)
                            rows = slice(a_ * 64, (a_ + 1) * 64)
                            DMA("sync", SACC[rows, d_, q, a_ * 64:(a_ + 1) * 64], st0[l, d_, h], w=["SACC%d" % d_])
                for step in range(bpr):
                    for d_ in range(2):
                        c = step if d_ == 0 else bpr - 1 - step
                        gb = gb0 + c
                        sk_ = "SACC%d" % d_
                        for a_ in range(2):
                            rows = slice(a_ * 64, (a_ + 1) * 64)
                            CP("scalar", SBF[rows, d_, c, :, :], SACC[rows, d_, :, a_ * 64:(a_ + 1) * 64], r=[sk_], w=["SBF"])
                        kz = KZ[(step * 2 + d_) % 2]
                        kzk = "KZ%d" % ((step * 2 + d_) % 2)
                        TT("vector", kz.rearrange("p (h d) -> p h d", h=4), KTOK[:, gb, :].rearrange("p (h d) -> p h d", h=4),
                           ZT[:, d_ * 4:(d_ + 1) * 4].unsqueeze(2).to_broadcast([128, 4, 64]), ALU.mult, r=["KTOK", "ZT"], w=[kzk])
                        sb_ = bank("sc")
                        for q in range(2):
                            MM(ps[sb_][:, q * 128:(q + 1) * 128], kz[:, q * 128:(q + 1) * 128], VTOK[:, gb, q * 128:(q + 1) * 128],
                               True, True, r=[kzk, "VTOK"], w=[pk(sb_)])
                        for q in range(2):
                            STT("vector", SACC[:, d_, q, :], SACC[:, d_, q, :], GCHP[:, d_, q:q + 1], ps[sb_][:, q * 128:(q + 1) * 128],
                                ALU.mult, ALU.add, r=[pk(sb_), "GCHP", sk_], w=[pk(sb_), sk_])
                if not lat:
                    for d_ in range(2):
                        for h in range(4):
                            q, a_ = divmod(h, 2)
                            rows = slice(a_ * 64, (a_ + 1) * 64)
                            DMA("gpsimd", ost[rq, l, d_, h], SACC[rows, d_, q, a_ * 64:(a_ + 1) * 64], r=["SACC%d" % d_])
                for c in range(bpr):
                    gb = gb0 + c
                    bc = slice(gb * 128, (gb + 1) * 128)
                    i2 = c % 2
                    sbs = [bank("sc"), bank("sc")]
                    for a_ in range(2):
                        rows = slice(a_ * 64, (a_ + 1) * 64)
                        for q in range(2):
                            MM(ps[sbs[a_]][:, q * 128:(q + 1) * 128], KTC[rows, q, bc], QTC[rows, q, bc], True, True,
                               r=["KTC", "QTC"], w=[pk(sbs[a_])])
                    for d_ in range(2):
                        for a_ in range(2):
                            TT("vector", PD[i2][:, d_, a_:4:2, :], ps[sbs[a_]][:, 0:256].rearrange("p (q i) -> p q i", q=2),
                               DTT[:, d_ * 4 + a_:d_ * 4 + 4:2, :], ALU.mult, r=[pk(sbs[a_]), "DTT"], w=[pk(sbs[a_]), "PD%d" % i2])
                        TT("gpsimd", QX[i2][:, d_], QTC[:, :, bc], XIP[:, d_], ALU.mult, r=["QTC", "XIP"], w=["QX%d" % i2])
                    ob = obank()
                    for h in range(4):
                        q, a_ = divmod(h, 2)
                        rows = slice(a_ * 64, (a_ + 1) * 64)
                        oc = ps[ob][:, h * 64:(h + 1) * 64]
                        rk = ["PD%d" % i2, "QX%d" % i2, "VTOK", "SBF"]
                        MM(oc, PD[i2][:, 0, h, :], VTOK[:, gb, h * 64:(h + 1) * 64], True, False, r=rk, w=[pk(ob)])
                        MM(oc, QX[i2][rows, 0, q, :], SBF[rows, 0, c, q, :], False, False, r=rk, w=[pk(ob)])
                        MM(oc, PD[i2][:, 1, h, :], VTOK[:, gb, h * 64:(h + 1) * 64], False, False, r=rk, w=[pk(ob)])
                        MM(oc, QX[i2][rows, 1, q, :], SBF[rows, 1, c, q, :], False, True, r=rk, w=[pk(ob)])
                    o3 = O32.rearrange("p (h d) -> p h d", h=4)
                    CP("scalar", O32, ps[ob][:, 0:256], r=[pk(ob)], w=[pk(ob), "O32"])
                    RED("vector", GST[:, 0:4], o3, ALU.add, r=["O32"], w=["GSa"])
                    TT("gpsimd", SQ, O32, O32, ALU.mult, r=["O32"], w=["SQ"])
                    RED("vector", GST[:, 4:8], SQ.rearrange("p (h d) -> p h d", h=4), ALU.add, r=["SQ"], w=["GSb"])
                    TS("vector", GST[:, 0:4], GST[:, 0:4], 1.0 / 64, None, ALU.mult, None, r=["GSa"], w=["GSa"])
                    TT("vector", GST[:, 8:12], GST[:, 0:4], GST[:, 0:4], ALU.mult, r=["GSa"], w=["GSc"])
                    STT("vector", GST[:, 4:8], GST[:, 4:8], 1.0 / 64, GST[:, 8:12], ALU.mult, ALU.subtract, r=["GSb", "GSc"], w=["GSb"])
                    ACT(GST[:, 4:8], GST[:, 4:8], AF.Ln, r=["GSb", "EPSB"], w=["GSb"], bias=EPSB)
                    ACT(GST[:, 4:8], GST[:, 4:8], AF.Exp, r=["GSb"], w=["GSb"], scale=-0.5)
                    TT("vector", o3, o3, GST[:, 0:4].unsqueeze(2).to_broadcast([128, 4, 64]), ALU.subtract, r=["O32", "GSa"], w=["O32"])
                    TT("vector", o3, o3, GST[:, 4:8].unsqueeze(2).to_broadcast([128, 4, 64]), ALU.mult, r=["O32", "GSb"], w=["O32"])
                    TT("gpsimd", O32, O32, GW, ALU.mult, r=["O32", "GW"], w=["O32"])
                    cs = CS[c % 2]
                    csk = "CS%d" % (c % 2)
                    TT("gpsimd", cs[:, 0, 0:256], O32, GS[:, gb, :], ALU.mult, r=["O32", "GS"], w=[csk])
                    cs_to_catT(cs, csk, gb, 1, 2, 6)

        group_A()
        P.barrier()
        hmode[0] = "load"
        AR.top = GTOP
        for hp in range(2):
            group_B(hp)
            if hp == 1:
                P.barrier()
            AR.top = GTOP
        group_C()
        P.barrier()
        AR.top = GTOP

        DMA("sync", LNG, lnp_d[l, 0, :].partition_broadcast(128), w=["LNG"])
        DMA("sync", LNB, lnp_d[l, 1, :].partition_broadcast(128), w=["LNB"])
        wo0, ko0 = wload(wsrc(w_out[l], 0, 512))
        wo1, ko1 = wload(wsrc(w_out[l], 512, 1024))
        T1s = [AR.f32(D), AR.f32(D)]
        TNs = [AR.f32(D), AR.f32(D)]
        XO = [AR.f32(D), AR.f32(D)]
        LSTs = [AR.f32(16), AR.f32(16)]
        RT = AR.f32(64)
        CTB = [AR.bf(D), AR.bf(D)]

        def op_stage1(b):
            ctb = CTB[b % 2]
            ctk = "CTB%d" % (b % 2)
            T1 = T1s[b % 2]
            t1k = "T1_%d" % (b % 2)
            DMA("sync", ctb, catd[b * 128:(b + 1) * 128, :], r=["catd"], w=[ctk])
            cth, cthk = to_hT(ctb, ctk)
            for dh, (wo, ko) in enumerate(((wo0, ko0), (wo1, ko1))):
                bk = bank("mm")
                for k in range(8):
                    MM(ps[bk][:, :], cth[:, k, :], wo[:, k, :], k == 0, k == 7, r=[cthk] + ko, w=[pk(bk)])
                TT("vector", T1[:, dh * 512:(dh + 1) * 512], ps[bk][:, :], M2[:, dh * 512:(dh + 1) * 512], ALU.mult,
                   r=[pk(bk), "M2"], w=[pk(bk), t1k])

        st2 = {}

        def op_stage2a(b):
            T1 = T1s[b % 2]
            t1k = "T1_%d" % (b % 2)
            xap, xk = xblock(b)
            ln_front(T1, t1k, xap, xk, LSTs[b % 2], "%d" % (b % 2))

        def op_stage2b(b):
            T1 = T1s[b % 2]
            t1k = "T1_%d" % (b % 2)
            if lat:
                xo = XO[b % 2]
                xok = ["XO%d" % (b % 2)]
            else:
                xo = XC[:, b, :]
                xok = ["XC%d" % b]
            ln_back(T1, t1k, xo, xok, TNs[b % 2], LSTs[b % 2], "%d" % (b % 2))
            if lat:
                DMA("gpsimd", xls[b * 128:(b + 1) * 128, :], xo, r=xok, w=["xls"])
            moe_input(xo, xok, S.gb0 + b, HF, HB, kctr, to_hT, RT)

        op_stage1(0)
        op_stage2a(0)
        for b in range(nblk):
            if b + 1 < nblk:
                op_stage1(b + 1)
                op_stage2a(b + 1)
            op_stage2b(b)

    def ln_front(T1, tkey, xap, xk, LST, sfx=""):
        stats = LST[:, 0:12]
        mv = LST[:, 12:14]
        rstd = LST[:, 14:15]
        nmr = LST[:, 15:16]
        STT("vector", T1, xap, ALPHA, T1, ALU.mult, ALU.add, r=[tkey] + xk, w=[tkey])
        P.op("vector", lambda e: e.bn_stats(out=stats[:, 0:6], in_=T1[:, 0:512]), [tkey], [("lnst" + sfx)])
        P.op("vector", lambda e: e.bn_stats(out=stats[:, 6:12], in_=T1[:, 512:1024]), [tkey], [("lnst2" + sfx)])
        P.op("vector", lambda e: e.bn_aggr(out=mv, in_=stats), [("lnst" + sfx), ("lnst2" + sfx)], [("lnmv" + sfx)])
        ACT(rstd, mv[:, 1:2], AF.Ln, r=[("lnmv" + sfx), "EPSB"], w=[("lnrs" + sfx)], bias=EPSB)
        ACT(rstd, rstd, AF.Exp, r=[("lnrs" + sfx)], w=[("lnrs" + sfx)], scale=-0.5)
        STT("vector", nmr, mv[:, 0:1], -1.0, rstd, ALU.mult, ALU.mult, r=[("lnmv" + sfx), ("lnrs" + sfx)], w=[("lnnm" + sfx)])

    def ln_back(T1, tkey, out_ap, out_keys, TN, LST, sfx=""):
        rstd = LST[:, 14:15]
        nmr = LST[:, 15:16]
        ACT(TN, T1, AF.Identity, r=[tkey, ("lnrs" + sfx), ("lnnm" + sfx)], w=[("TN" + sfx)], bias=nmr, scale=rstd)
        TT("vector", TN, TN, LNG, ALU.mult, r=[("TN" + sfx), "LNG"], w=[("TN" + sfx)])
        TT("vector", out_ap, TN, LNB, ALU.add, r=[("TN" + sfx), "LNB"], w=out_keys)

    def moe_input(xo, xok, gb, HF, HB, kctr, to_hT, RT):
        i = kctr[0] % 2
        kctr[0] += 1
        TT("gpsimd", HF, xo, M0, ALU.mult, r=xok + ["M0"], w=["HF"])
        TT("vector", HB[i], HF, M1, ALU.add, r=["HF", "M1"], w=["HB%d" % i])
        DMA("gpsimd", hbuf[gb * 128:(gb + 1) * 128, :], HB[i], r=["HB%d" % i], w=["hbuf"])
        ht, htk = to_hT(HB[i], "HB%d" % i)
        bk = bank("mm")
        for k in range(8):
            MM(ps[bk][:, 0:16], ht[:, k, :], WRB[:, k, :], k == 0, k == 7, r=[htk, "WRB"], w=[pk(bk)])
        NMX = RT[:, 0:1]
        SSUM = RT[:, 1:2]
        EX = RT[:, 16:32]
        RED("vector", NMX, ps[bk][:, 0:16], ALU.max, r=[pk(bk)], w=[pk(bk), "NMX"])
        TS("vector", NMX, NMX, -1.0, None, ALU.mult, None, r=["NMX"], w=["NMX"])
        ACT(EX, ps[bk][:, 0:16], AF.Exp, r=[pk(bk), "NMX"], w=[pk(bk), "EX", "SSUM"], bias=NMX, accum=SSUM)
        P.op("vector", lambda e: e.reciprocal(out=SSUM, in_=SSUM), ["SSUM"], ["SSUM"])
        TS("vector", AFF[:, gb, :], EX, SSUM, None, ALU.mult, None, r=["EX", "SSUM"], w=["AFF"])


    def moe(l, do_lat):
        reqs = [([0, 1], CAP_C), ([2, 3], CAP_C)]
        if do_lat:
            reqs.append((list(range(4, 20)), CAP_L))
        ngb = 20 if do_lat else 4
        nsc = 3 if do_lat else 1
        NS = nsc * 128
        alloc_wsl(6)
        del WST[2:]
        WST.append(MREG[:, 0:2048].rearrange("p (k c) -> p k c", k=8))
        WST.append(MREG[:, 2048:4096].rearrange("p (k c) -> p k c", k=8))
        AT = AR.f32(ngb * 128)
        WK = AR.f32(ngb * 128)
        M8 = AR.f32(8)
        THR = AR.f32(1)
        MASKF = AR.f32(ngb * 16).rearrange("p (g e) -> p g e", g=ngb)
        MASKB = AR.bf(ngb * 16).rearrange("p (g e) -> p g e", g=ngb)
        RANK = AR.f32(ngb * 16).rearrange("p (g e) -> p g e", g=ngb)
        TG = AR.bf(ngb * 16 * 4).rearrange("p (g e f) -> p g e f", g=ngb, e=16)
        TGF = TG.rearrange("p g e f -> p (g e) f")
        AHI = AR.f32(ngb * 16)
        AFFF = AFF[:, 0:ngb, :].rearrange("p g e -> p (g e)")
        ZT1 = AR.f32(D)
        PSC = [AR.bf(4 * 64).rearrange("p (g c) -> p g c", g=4) for _ in range(2)]
        PSL = [AR.bf(16 * 256).rearrange("p (g c) -> p g c", g=16) for _ in range(2)] if do_lat else None
        IDT = [AR.f32(12).rearrange("p (s f) -> p s f", s=3) for _ in range(2)]
        IG = [AR.f32(3) for _ in range(2)]
        GATE = [AR.f32(3) for _ in range(2)]
        IDXG = [AR.i32(3) for _ in range(2)]
        IDXS = [AR.i32(3) for _ in range(2)]
        XE = [AR.bf(3 * D).rearrange("p (s d) -> p s d", s=3) for _ in range(2)]
        XET = AR.bf(8 * 384).rearrange("p (k c) -> p k c", k=8)
        SA = [AR.f32(384), AR.f32(384)]
        ACTT = AR.bf(8 * 384).rearrange("p (k c) -> p k c", k=8)
        YE = [AR.f32(D), AR.f32(D)]

        MS("gpsimd", ZT1, 0.0, ["ZT1"])
        for gb in range(ngb):
            DMA("gpsimd", ysc[gb * 128:(gb + 1) * 128, :], ZT1, r=["ZT1"], w=["ysc"])
        for g0 in range(0, ngb, 4):
            tb_ = bank("tr")
            for i in range(4):
                TR(ps[tb_][0:16, i * 128:(i + 1) * 128], AFF[:, g0 + i, :], IDF, r=["AFF", "CF"], w=[pk(tb_)])
            CP("vector", AT[0:16, g0 * 128:(g0 + 4) * 128], ps[tb_][0:16, 0:512], r=[pk(tb_)], w=[pk(tb_), "AT"])
        for ri, (blks, cap) in enumerate(reqs):
            c0, c1 = blks[0] * 128, (blks[-1] + 1) * 128
            wkk = "WK%d" % ri
            CP("gpsimd", WK[0:16, c0:c1], AT[0:16, c0:c1], r=["AT"], w=[wkk])
            rounds = cap // 8
            for r_ in range(rounds):
                P.op("vector", lambda e, c0=c0, c1=c1: e.max(out=M8[0:16, 0:8], in_=WK[0:16, c0:c1]), [wkk], ["M8"])
                if r_ < rounds - 1:
                    P.op("vector", lambda e, c0=c0, c1=c1: e.match_replace(out=WK[0:16, c0:c1], in_to_replace=M8[0:16, 0:8],
                                                                          in_values=WK[0:16, c0:c1], imm_value=-1.0), ["M8", wkk], [wkk])
            RED("vector", THR[0:16, 0:1], M8[0:16, 0:8], ALU.min, r=["M8"], w=["THR"])
            TS("vector", WK[0:16, c0:c1], AT[0:16, c0:c1], THR[0:16, 0:1], None, ALU.is_ge, None, r=["AT", "THR", wkk], w=[wkk])
        allwk = ["WK%d" % ri for ri in range(len(reqs))]
        mb = bank("mm")
        for gb in range(ngb):
            TR(ps[mb][:, gb * 16:(gb + 1) * 16], WK[0:16, gb * 128:(gb + 1) * 128], IDF[0:16, 0:16], r=allwk + ["CF"], w=[pk(mb)])
        CP("vector", MASKF, ps[mb][:, 0:ngb * 16].rearrange("p (g e) -> p g e", g=ngb), r=[pk(mb)], w=[pk(mb), "MASKF"])
        CP("gpsimd", MASKB, MASKF, r=["MASKF"], w=["MASKB"])
        for (blks, cap) in reqs:
            for bi, gb in enumerate(blks):
                rb = bank("mm")
                for bj in range(bi):
                    MM(ps[rb][:, 0:16], ONESB, MASKB[:, blks[bj], :], bj == 0, False, r=["MASKB", "CB"], w=[pk(rb)])
                MM(ps[rb][:, 0:16], SUB, MASKB[:, gb, :], bi == 0, True, r=["MASKB", "CB"], w=[pk(rb)])
                CP("vector", RANK[:, gb, :], ps[rb][:, 0:16], r=[pk(rb)], w=[pk(rb), "RANK"])
        for gb in range(ngb):
            MS("gpsimd", TG[:, gb, :, 0], float(gb), ["TG0"])
        CP("gpsimd", TGF[:, :, 1], CF[:, C_PLO:C_PLO + 1].to_broadcast([128, ngb * 16]), r=["CF"], w=["TG1"])
        CP("vector", TGF[:, :, 2], AFFF, r=["AFF"], w=["TG2"])
        CP("vector", AHI, TGF[:, :, 2], r=["TG2"], w=["AHI"])
        TT("vector", TGF[:, :, 3], AFFF, AHI, ALU.subtract, r=["AFF", "AHI"], w=["TG3"])
        TGK = ["TG0", "TG1", "TG2", "TG3"]
        for pe in range(2):
            MS("vector", PSC[pe], 0.0, ["PSC%d" % pe])
            MS("vector", IDXG[pe], 0, ["IDXG%d" % pe])
            CP("vector", IDXS[pe][:, 0:1], CF[:, C_PADS:C_PADS + 1], r=["CF"], w=["IDXS%d" % pe])
            MS("vector", GATE[pe], 0.0, ["GATE%d" % pe])
            MS("vector", IG[pe], 0.0, ["IG%d" % pe])
            MS("vector", IDT[pe], 0.0, ["IDT%d" % pe])

        wts = {}

        def wl(e_, name):
            if e_ >= E:
                return
            if name[0] == "D":
                c0 = int(name[1]) * 512
                wts[(e_, name)] = wload(wsrc(w_dn[l, e_], c0, c0 + 512))
            else:
                c0 = (0 if name[0] == "A" else DFF) + int(name[1]) * 512
                wts[(e_, name)] = wload(wsrc(w_gu[l, e_], c0, c0 + 512))

        def prep_a(e_):
            pe = e_ % 2
            sf = "%d" % pe
            for gb in range(4):
                off = 0 if gb < 2 else 32
                TS("vector", PSC[pe][:, gb, off:off + 32], CF[:, C_IOTA:C_IOTA + 32], RANK[:, gb, e_:e_ + 1], MASKF[:, gb, e_:e_ + 1],
                   ALU.is_equal, ALU.mult, r=["CF", "RANK", "MASKF"], w=["PSC" + sf])
            if do_lat:
                for bi in range(16):
                    TS("vector", PSL[pe][:, bi, :], CF[:, C_IOTA:C_IOTA + 256], RANK[:, 4 + bi, e_:e_ + 1], MASKF[:, 4 + bi, e_:e_ + 1],
                       ALU.is_equal, ALU.mult, r=["CF", "RANK", "MASKF"], w=["PSL%s_%d" % (sf, bi)])

        def prep_a2(e_):
            pe = e_ % 2
            sf = "%d" % pe
            ib = bank("mm")
            for gb in range(4):
                MM(ps[ib][0:64, 0:4], PSC[pe][:, gb, :], TG[:, gb, e_, :], gb == 0, gb == 3, r=["PSC" + sf] + TGK, w=[pk(ib)])
            if do_lat:
                for cch in range(2):
                    for bi in range(16):
                        MM(ps[ib][:, 4 + cch * 4:8 + cch * 4], PSL[pe][:, bi, cch * 128:(cch + 1) * 128], TG[:, 4 + bi, e_, :],
                           bi == 0, bi == 15, r=["PSL%s_%d" % (sf, bi)] + TGK, w=[pk(ib)])
            idt, ig, gt, ixg, ixs = IDT[pe], IG[pe], GATE[pe], IDXG[pe], IDXS[pe]
            CP("vector", idt[0:64, 0, :], ps[ib][0:64, 0:4], r=[pk(ib)], w=[pk(ib), "IDT" + sf])
            if do_lat:
                CP("vector", idt[:, 1:3, :], ps[ib][:, 4:12].rearrange("p (s f) -> p s f", s=2), r=[pk(ib)], w=[pk(ib), "IDT" + sf])
            STT("vector", ig[0:64, 0:1], idt[0:64, 0, 0:1], 128.0, idt[0:64, 0, 1:2], ALU.mult, ALU.add, r=["IDT" + sf], w=["IG" + sf])
            TT("vector", gt[0:64, 0:1], idt[0:64, 0, 2:3], idt[0:64, 0, 3:4], ALU.add, r=["IDT" + sf], w=["GATE" + sf])
            if do_lat:
                STT("vector", ig[:, 1:3], idt[:, 1:3, 0], 128.0, idt[:, 1:3, 1], ALU.mult, ALU.add, r=["IDT" + sf], w=["IG" + sf])
                TT("vector", gt[:, 1:3], idt[:, 1:3, 2], idt[:, 1:3, 3], ALU.add, r=["IDT" + sf], w=["GATE" + sf])
            CP("vector", ixg[0:64, 0:1], ig[0:64, 0:1], r=["IG" + sf], w=["IDXG" + sf])
            CP("vector", ixs[0:64, 0:1], ig[0:64, 0:1], r=["IG" + sf], w=["IDXS" + sf])
            if do_lat:
                CP("vector", ixg[:, 1:3], ig[:, 1:3], r=["IG" + sf], w=["IDXG" + sf])
                CP("vector", ixs[:, 1:3], ig[:, 1:3], r=["IG" + sf], w=["IDXS" + sf])
            for sc in range(nsc):
                P.op("gpsimd", lambda e, sc=sc, pe=pe, ixg=ixg: e.indirect_dma_start(
                    out=XE[pe][:, sc, :], out_offset=None, in_=hbuf[0:ngb * 128, :],
                    in_offset=bass.IndirectOffsetOnAxis(ixg[:, sc:sc + 1], 0)),
                    ["IDXG" + sf, "hbuf"], ["XE%s_%d" % (sf, sc)], dma=True)

        def prep_b(e_):
            pe = e_ % 2
            sf = "%d" % pe
            for k in range(8):
                tb_ = bank("tr")
                for sc in range(nsc):
                    TR(psb[tb_][:, sc * 128:(sc + 1) * 128], XE[pe][:, sc, k * 128:(k + 1) * 128], IDB,
                       r=["XE%s_%d" % (sf, sc), "CB"], w=[pk(tb_)])
                CP("vector" if k % 2 else "scalar", XET[:, k, 0:NS], psb[tb_][:, 0:NS], r=[pk(tb_)], w=[pk(tb_), "XET"])

        for name in ("A0", "B0", "A1", "B1", "D0", "D1"):
            wl(0, name)
        prep_a(0)
        prep_a2(0)
        for e_ in range(E):
            pe = e_ % 2
            sf = "%d" % pe
            if e_ + 1 < E:
                prep_a(e_ + 1)
            prep_b(e_)
            for fc in range(8):
                ga, gak = wts[(e_, "A%d" % (fc // 4))]
                gb_, gbk = wts[(e_, "B%d" % (fc // 4))]
                o = (fc % 4) * 128
                pa = bank("mm")
                for k in range(8):
                    MM(ps[pa][:, 0:NS], ga[:, k, o:o + 128], XET[:, k, 0:NS], k == 0, k == 7, r=gak + ["XET"], w=[pk(pa)])
                pb_ = bank("sc")
                for k in range(8):
                    MM(ps[pb_][:, 0:NS], gb_[:, k, o:o + 128], XET[:, k, 0:NS], k == 0, k == 7, r=gbk + ["XET"], w=[pk(pb_)])
                sa = SA[fc % 2]
                sak = "SA%d" % (fc % 2)
                ACT(sa[:, 0:NS], ps[pa][:, 0:NS], AF.Silu, r=[pk(pa)], w=[pk(pa), sak])
                TT("vector", ACTT[:, fc, 0:NS], ps[pb_][:, 0:NS], sa[:, 0:NS], ALU.mult, r=[pk(pb_), sak], w=[pk(pb_), "ACTT%d" % fc])
                if fc == 3:
                    wl(e_ + 1, "A0")
                    wl(e_ + 1, "B0")
                    if e_ + 1 < E:
                        prep_a2(e_ + 1)
                if fc == 7:
                    wl(e_ + 1, "A1")
                    wl(e_ + 1, "B1")
            for sc in range(nsc):
                ye = YE[sc % 2]
                yek = "YE%d" % (sc % 2)
                for dh in range(2):
                    dw, dwk = wts[(e_, "D%d" % dh)]
                    pd_ = bank("mm")
                    for fc in range(8):
                        MM(ps[pd_][:, :], ACTT[:, fc, sc * 128:(sc + 1) * 128], dw[:, fc, :], fc == 0, fc == 7,
                           r=["ACTT%d" % fc] + dwk, w=[pk(pd_)])
                    TS("vector", ye[:, dh * 512:(dh + 1) * 512], ps[pd_][:, :], GATE[pe][:, sc:sc + 1], None, ALU.mult, None,
                       r=[pk(pd_), "GATE" + sf], w=[pk(pd_), yek])
                P.op("gpsimd", lambda e, sc=sc, ye=ye, pe=pe: e.indirect_dma_start(
                    out=ysc[:, :], out_offset=bass.IndirectOffsetOnAxis(IDXS[pe][:, sc:sc + 1], 0),
                    in_=ye, in_offset=None, compute_op=ALU.add),
                    ["IDXS" + sf, yek], ["ysc", yek + "s"], dma=True)
            wl(e_ + 1, "D0")
            wl(e_ + 1, "D1")

    def ln2(l, do_lat, last):
        NB = 4
        T1s = [AR.f32(D) for _ in range(NB)]
        XI_ = [AR.f32(D) for _ in range(NB)]
        LSTs = [AR.f32(16) for _ in range(NB)]
        TNs = [AR.f32(D) for _ in range(NB)]
        XO = [AR.f32(D), AR.f32(D)]
        DMA("sync", LNG, lnp_d[l, 2, :].partition_broadcast(128), w=["LNG"])
        DMA("sync", LNB, lnp_d[l, 3, :].partition_broadcast(128), w=["LNB"])
        get_mod(M2, "M2", 0, l, 5, False)
        for b in range(4):
            i = b % NB
            T1 = T1s[i]
            tk = "T1_%d" % i
            DMA("sync", T1, ysc[b * 128:(b + 1) * 128, :], r=["ysc"], w=[tk])
            TT("vector", T1, T1, M2, ALU.mult, r=[tk, "M2"], w=[tk])
            ln_front(T1, tk, XC[:, b, :], ["XC%d" % b], LSTs[i], "%d" % i)
        for b in range(4):
            i = b % NB
            ln_back(T1s[i], "T1_%d" % i, XC[:, b, :], ["XC%d" % b], TNs[i], LSTs[i], "%d" % i)
            if last:
                DMA("gpsimd", yc[b * 128:(b + 1) * 128, :], XC[:, b, :], r=["XC%d" % b])
        if do_lat:
            get_mod(M2, "M2", 1, l, 5, False)

            def l2_front(b):
                i = b % NB
                T1 = T1s[i]
                tk = "T1_%d" % i
                DMA("sync", T1, ysc[(4 + b) * 128:(5 + b) * 128, :], r=["ysc"], w=[tk])
                DMA("sync", XI_[i], xls[b * 128:(b + 1) * 128, :], r=["xls"], w=["XI%d" % i])
                TT("vector", T1, T1, M2, ALU.mult, r=[tk, "M2"], w=[tk])
                ln_front(T1, tk, XI_[i], ["XI%d" % i], LSTs[i], "%d" % i)

            def l2_back(b):
                i = b % NB
                j = b % 2
                ln_back(T1s[i], "T1_%d" % i, XO[j], ["XO%d" % j], TNs[i], LSTs[i], "%d" % i)
                dst = yl if last else xls
                DMA("gpsimd", dst[b * 128:(b + 1) * 128, :], XO[j], r=["XO%d" % j], w=["xls"])

            SK = 2
            for b in range(SK):
                l2_front(b)
            for b in range(16):
                if b + SK < 16:
                    l2_front(b + SK)
                l2_back(b)

    for l in range(n_layers):
        DMA("sync", ESINK, sink[l, :].partition_broadcast(128), w=["ESINK"])
        ACT(ESINK, ESINK, AF.Exp, r=["ESINK"], w=["ESINK"])
        DMA("sync", GW, gn[l, :].partition_broadcast(128), w=["GW"])
        DMA("sync", WRF, w_r[l].rearrange("(k p) e -> p k e", p=128), w=["WRF"])
        CP("vector", WRB, WRF, r=["WRF"], w=["WRB"])
        mixer(SC_, l)
        P.barrier()
        del WST[2:]
        AR.top = PERSIST_TOP
        if do_lat:
            mixer(SL_, l)
            P.barrier()
            del WST[2:]
            AR.top = PERSIST_TOP
        moe(l, do_lat)
        P.barrier()
        del WST[2:]
        AR.top = PERSIST_TOP
        alloc_wsl(3)
        ln2(l, do_lat, l == n_layers - 1)
        P.barrier()
        AR.top = PERSIST_TOP
    build_nc.last_prog = P
    P.emit(nc, max_ops)
    stack.close()
    return nc


def _rearrange_w_in(w_in):
    qa, ka, va = w_in[:, :, 0:512], w_in[:, :, 512:640], w_in[:, :, 640:768]
    qb, kb, vb = w_in[:, :, 768:1024], w_in[:, :, 1024:1280], w_in[:, :, 1280:1536]
    rest = w_in[:, :, 1536:2560]
    ka0, ka1 = ka[:, :, 0:64], ka[:, :, 64:128]
    cols = [qa, ka0, ka0, ka1, ka1, va]
    for hp in range(2):
        s = slice(hp * 128, (hp + 1) * 128)
        cols += [qb[:, :, s], kb[:, :, s], vb[:, :, s]]
    cols.append(rest)
    out = np.ascontiguousarray(np.concatenate(cols, axis=-1))
    assert out.shape[-1] == NCOL
    return out


_NC_CACHE = {}


def _get_nc(n_layers=DEPTH, do_lat=True):
    key = (n_layers, do_lat)
    if key not in _NC_CACHE:
        _NC_CACHE[key] = build_nc(n_layers, do_lat)
    return _NC_CACHE[key]


def make_in_maps(inputs, n_cores=8):
    f = lambda a: np.ascontiguousarray(np.asarray(a, dtype=np.float32))
    I = {k: f(v) for k, v in inputs.items()}
    w_in_r = _rearrange_w_in(I["w_in"])
    tb = _expand_rpb(I["na_rpb"])
    dec = I["ret_decay"].reshape(DEPTH, 8)
    lnp = np.ascontiguousarray(np.stack([I["ln1_g"], I["ln1_b"], I["ln2_g"], I["ln2_b"]], axis=1))
    cf = _consts()
    ropec, ropes = _rope_tables()
    augk, augq = _aug_tables()
    shared = {
        "w_ada": I["w_ada"], "b_ada": I["b_ada"], "w_in": w_in_r, "w_out": I["w_out"], "w_r": I["w_router"],
        "w_gu": I["w_gate_up"], "w_dn": I["w_down"], "sink": I["attn_sink"], "tb": tb, "dec": np.ascontiguousarray(dec),
        "gn": I["ret_gn"], "lnp": lnp, "cf": cf, "augk": augk, "augq": augq, "ropec": ropec, "ropes": ropes,
    }
    maps = []
    for i in range(n_cores):
        b = i % 4
        m = dict(shared)
        m["xc"] = np.ascontiguousarray(I["x_prompt"][2 * i:2 * i + 2].reshape(512, D))
        m["xl"] = np.ascontiguousarray(I["x_sample"][b])
        m["cond"] = np.ascontiguousarray(np.stack([I["c_ctx"], I["c"][b]], axis=0))
        m["cak"] = np.ascontiguousarray(I["cache_attn_a_k"][b].reshape(DEPTH, PAST, 128))
        m["cav"] = np.ascontiguousarray(I["cache_attn_a_v"][b].reshape(DEPTH, PAST, 128))
        m["cbk"] = np.ascontiguousarray(I["cache_attn_b_k"][b].reshape(DEPTH, PAST, 256))
        m["cbv"] = np.ascontiguousarray(I["cache_attn_b_v"][b].reshape(DEPTH, PAST, 256))
        m["st0"] = np.ascontiguousarray(I["state_ret"][b])
        maps.append(m)
    return maps


def assemble(results):
    y_prompt = np.zeros((16, NCTX, D), np.float32)
    y_sample = np.zeros((4, NL, D), np.float32)
    a_k = np.zeros((16, DEPTH, NCTX, 2, 64), np.float32)
    a_v = np.zeros((16, DEPTH, NCTX, 2, 64), np.float32)
    b_k = np.zeros((16, DEPTH, NCTX, 4, 64), np.float32)
    b_v = np.zeros((16, DEPTH, NCTX, 4, 64), np.float32)
    st = np.zeros((16, DEPTH, 2, 4, 64, 64), np.float32)
    for i, r in enumerate(results):
        y_prompt[2 * i:2 * i + 2] = np.asarray(r["yc"]).reshape(2, NCTX, D)
        if i < 4:
            y_sample[i] = np.asarray(r["yl"])
        a_k[2 * i:2 * i + 2] = np.asarray(r["oak"]).reshape(2, DEPTH, NCTX, 2, 64)
        a_v[2 * i:2 * i + 2] = np.asarray(r["oav"]).reshape(2, DEPTH, NCTX, 2, 64)
        b_k[2 * i:2 * i + 2] = np.asarray(r["obk"]).reshape(2, DEPTH, NCTX, 4, 64)
        b_v[2 * i:2 * i + 2] = np.asarray(r["obv"]).reshape(2, DEPTH, NCTX, 4, 64)
        st[2 * i:2 * i + 2] = np.asarray(r["ost"])
    return (y_prompt, y_sample, a_k, a_v, b_k, b_v, st)


def kernel(**inputs):
    nc = _get_nc()
    maps = make_in_maps(inputs)
    res = run_bass_kernel_spmd(nc, maps, core_ids=list(range(8)))
    return assemble(res.results)
```

```python
import contextlib
import sys
import numpy as np
import concourse.bass as bass
import concourse.mybir as mybir
from concourse.bass_utils import run_bass_kernel_spmd

F32 = mybir.dt.float32
BF16 = mybir.dt.bfloat16
I32 = mybir.dt.int32
AF = mybir.ActivationFunctionType
ALU = mybir.AluOpType
AX = mybir.AxisListType

ENGS = ("tensor", "vector", "scalar", "gpsimd", "sync")
NDMA_HW = 40
NDMA_SW = 8
NDMA_SEMS = NDMA_HW + NDMA_SW


class Op:
    __slots__ = ("eng", "fn", "reads", "writes", "is_dma", "waits", "inc", "dma_sem", "dma_val", "prewait", "swdma", "line")

    def __init__(self, eng, fn, reads, writes, is_dma):
        self.eng = eng
        self.fn = fn
        self.reads = reads
        self.writes = writes
        self.is_dma = is_dma
        self.waits = {}
        self.inc = False
        self.dma_sem = None
        self.dma_val = None
        self.prewait = []
        self.swdma = False


class Prog:
    def __init__(self):
        self.ops = []
        self.eng_ops = {e: [] for e in ENGS}
        self.last_w = {}
        self.readers = {}
        self.dma_count = 0
        self.swdma_count = 0
        self.dma_sem_uses = [0] * NDMA_SEMS
        self.dma_sem_last = [None] * NDMA_SEMS

    def op(self, eng, fn, reads=(), writes=(), dma=False):
        sw = dma and eng == "gpsimd"
        o = Op(eng, fn, tuple(reads), tuple(writes), dma)
        o.swdma = sw
        deps = []
        for k in o.reads:
            w = self.last_w.get(k)
            if w is not None:
                deps.append(w)
        for k in o.writes:
            w = self.last_w.get(k)
            if w is not None:
                deps.append(w)
            deps.extend(self.readers.get(k, ()))
        if dma:
            if sw:
                j = NDMA_HW + (self.swdma_count % NDMA_SW)
                self.swdma_count += 1
            else:
                j = self.dma_count % NDMA_HW
                self.dma_count += 1
            prev = self.dma_sem_last[j]
            if prev is not None:
                deps.append(prev)
            self.dma_sem_uses[j] += 1
            o.dma_sem = j
            o.dma_val = 16 * self.dma_sem_uses[j]
            self.dma_sem_last[j] = o
        o.prewait = deps
        o.line = sys._getframe(2).f_lineno
        for k in o.reads:
            self.readers.setdefault(k, []).append(o)
        for k in o.writes:
            self.last_w[k] = o
            self.readers[k] = []
        self.eng_ops[eng].append(o)
        self.ops.append(o)
        return o

    def barrier(self):
        lasts = [self.eng_ops[e][-1] for e in ENGS if self.eng_ops[e]]
        lasts += [d for d in self.dma_sem_last if d is not None]
        for e in ENGS:
            o = Op(e, None, (), (), False)
            o.prewait = list(lasts)
            self.eng_ops[e].append(o)
            self.ops.append(o)

    @staticmethod
    def _needs_wait(d, o):
        if d.is_dma or o.is_dma:
            return True
        if d.eng != o.eng:
            return True
        return d.eng != "tensor"

    def finalize(self):
        for o in self.ops:
            for d in o.prewait:
                if d.fn is None:
                    continue
                if not d.is_dma and self._needs_wait(d, o):
                    d.inc = True
        self.inc_val = {}
        for e in ENGS:
            c = 0
            for o in self.eng_ops[e]:
                if o.inc and not o.is_dma:
                    c += 1
                    self.inc_val[id(o)] = c
        for o in self.ops:
            need = {}
            for d in o.prewait:
                if d.fn is None:
                    continue
                if d.is_dma:
                    key = ("dma", d.dma_sem)
                    v = d.dma_val
                else:
                    if not self._needs_wait(d, o):
                        continue
                    key = ("eng", d.eng)
                    v = self.inc_val[id(d)]
                if need.get(key, 0) < v:
                    need[key] = v
            o.waits = need
        for e in ENGS:
            seen = {}
            for o in self.eng_ops[e]:
                w2 = {}
                for k, v in o.waits.items():
                    if seen.get(k, 0) >= v:
                        continue
                    seen[k] = v
                    w2[k] = v
                o.waits = w2

    def emit(self, nc, max_ops=None):
        if max_ops is not None:
            keep = set(id(o) for o in self.ops[:max_ops])
            self.ops = self.ops[:max_ops]
            for e in ENGS:
                self.eng_ops[e] = [o for o in self.eng_ops[e] if id(o) in keep]
        self.finalize()
        with contextlib.ExitStack() as st:
            sems = {}
            for e in ENGS:
                sems[("eng", e)] = st.enter_context(nc.semaphore("s_" + e))
            for j in range(NDMA_SEMS):
                sems[("dma", j)] = st.enter_context(nc.semaphore("s_dma%d" % j))
            block = st.enter_context(nc.Block())
            prog = self

            def run(e, engobj):
                for o in prog.eng_ops[e]:
                    for k, v in o.waits.items():
                        engobj.wait_ge(sems[k], v)
                    if o.fn is None:
                        continue
                    ins = o.fn(engobj)
                    if o.is_dma:
                        ins.then_inc(sems[("dma", o.dma_sem)], 16)
                    elif o.inc:
                        ins.then_inc(sems[("eng", e)], 1)
                last = {}
                for o in prog.eng_ops[e]:
                    if o.is_dma:
                        last[o.dma_sem] = max(last.get(o.dma_sem, 0), o.dma_val)
                for j, v in last.items():
                    engobj.wait_ge(sems[("dma", j)], v)

            @block.sync
            def _(eng):
                run("sync", eng)

            @block.tensor
            def _(eng):
                run("tensor", eng)

            @block.vector
            def _(eng):
                run("vector", eng)

            @block.scalar
            def _(eng):
                run("scalar", eng)

            @block.gpsimd
            def _(eng):
                run("gpsimd", eng)


D = 1024
DEPTH = 2
NCTX = 256
NL = 2048
PAST = 512
GRID_W = 64
E = 16
DFF = 1024
NCOL = 2688
ALPHA = float((2 * DEPTH) ** 0.25)
EPS = 1e-6
NEG = -30000.0
NTOK = 512 + NL
NGB = NTOK // 128
CAP_C = 32
CAP_L = 256
NSLOT = 384

C_ID = 0
C_TRI1 = 128
C_TRI2 = 256
C_SU = 384
C_ONE = 512
C_RELP = 640
C_MF = 768
C_RELN = 896
C_MB = 1024
C_XIF = 1152
C_XIB = 1280
C_IOTA = 1408
C_ZF = 1664
C_ZB = 1665
C_PLO = 1666
C_PADS = 1667
NCF = 1668


def _consts():
    c = np.zeros((128, NCF), np.float32)
    p = np.arange(128)[:, None].astype(np.float32)
    i = np.arange(128)[None, :].astype(np.float32)
    c[:, C_ID:C_ID + 128] = (p == i)
    c[:, C_TRI1:C_TRI1 + 128] = (i <= p)
    c[:, C_TRI2:C_TRI2 + 128] = (p <= i)
    c[:, C_SU:C_SU + 128] = (p < i)
    c[:, C_ONE:C_ONE + 128] = 1.0
    c[:, C_RELP:C_RELP + 128] = np.maximum(i - p, 0)
    c[:, C_MF:C_MF + 128] = 0.125 * (i >= p)
    c[:, C_RELN:C_RELN + 128] = np.maximum(p - i, 0)
    c[:, C_MB:C_MB + 128] = 0.125 * (p >= i)
    c[:, C_XIF:C_XIF + 128] = i + 1
    c[:, C_XIB:C_XIB + 128] = 128 - i
    c[:, C_IOTA:C_IOTA + 256] = np.arange(256)[None, :]
    c[:, C_ZF] = 127 - p[:, 0]
    c[:, C_ZB] = p[:, 0]
    c[:, C_PLO] = p[:, 0]
    c[:, C_PADS] = 2560 + p[:, 0]
    return c


def _rope_tables():
    t = np.arange(NL)
    row, col = t // GRID_W, t % GRID_W
    inv = 1.0 / (10000.0 ** (np.arange(0, 32, 2) / 32.0))
    ar = (row[:, None] * inv[None, :]).astype(np.float32)
    ac = (col[:, None] * inv[None, :]).astype(np.float32)
    cos = np.concatenate([np.cos(ar), np.cos(ar), np.cos(ac), np.cos(ac)], -1).astype(np.float32)
    sin = np.concatenate([-np.sin(ar), np.sin(ar), -np.sin(ac), np.sin(ac)], -1).astype(np.float32)
    return cos, sin


def _aug_tables():
    tok = np.arange(NL)
    r = tok // GRID_W
    kr = np.arange(32)[:, None]
    augk = (r[None, :] == kr).astype(np.float32)
    rs = np.clip(r - 4, 0, 24)
    valid = (kr >= rs[None, :]) & (kr < rs[None, :] + 8)
    augq = np.where(valid, 0.0, NEG).astype(np.float32)
    return augk, augq


def _expand_rpb(rpb):
    a = (np.arange(128) // 64)[:, None, None]
    kc = (np.arange(128) % 64)[:, None, None]
    m = (np.arange(22) - 3)[None, :, None]
    c = np.arange(64)[None, None, :]
    drp = m - a
    dr = 14 - drp
    dc = kc - c + 15
    cs = np.clip(c - 8, 0, 48)
    valid = (drp >= 0) & (drp <= 14) & (kc >= cs) & (kc < cs + 16)
    dr_c = np.clip(dr, 0, 14)
    dc_c = np.clip(dc, 0, 30)
    dr_b, dc_b, valid_b = np.broadcast_arrays(dr_c, dc_c, valid)
    out = np.where(valid_b[None, None], rpb[:, :, dr_b, dc_b], np.float32(NEG)).astype(np.float32)
    return out.reshape(rpb.shape[0], 4, 128, 22 * 64)


ARENA_W = 53000


class Arena:
    def __init__(self, ap):
        self.ap = ap
        self.top = 0

    def f32(self, n):
        off = self.top
        self.top += n
        assert self.top <= ARENA_W, ("SBUF arena overflow", self.top)
        return self.ap[:, off:off + n]

    def bf(self, nel):
        return self.f32((nel + 1) // 2).bitcast(BF16)

    def i32(self, n):
        return self.f32(n).bitcast(I32)


def build_nc(n_layers=DEPTH, do_lat=True, max_ops=None):
    nc = bass.Bass("TRN2", target_bir_lowering=False)

    def din(name, shape, dt=F32):
        return nc.dram_tensor(name, shape, dt, kind="ExternalInput").ap()

    def dout(name, shape):
        return nc.dram_tensor(name, shape, F32, kind="ExternalOutput").ap()

    xc = din("xc", [512, D])
    xl = din("xl", [NL, D])
    cond = din("cond", [2, D])
    w_ada = din("w_ada", [2, D, 6 * D])
    b_ada = din("b_ada", [2, 6 * D])
    w_in = din("w_in", [2, D, NCOL])
    w_out = din("w_out", [2, D, D])
    w_r = din("w_r", [2, D, E])
    w_gu = din("w_gu", [2, E, D, 2 * DFF])
    w_dn = din("w_dn", [2, E, DFF, D])
    sink = din("sink", [2, 8])
    tbd = din("tb", [2, 4, 128, 22 * 64])
    dec = din("dec", [2, 8])
    gn = din("gn", [2, 256])
    lnp_d = din("lnp", [2, 4, D])
    cak = din("cak", [2, PAST, 128])
    cav = din("cav", [2, PAST, 128])
    cbk = din("cbk", [2, PAST, 256])
    cbv = din("cbv", [2, PAST, 256])
    st0 = din("st0", [2, 2, 4, 64, 64])
    cfd = din("cf", [128, NCF])
    augk = din("augk", [32, NL])
    augq = din("augq", [32, NL])
    ropec = din("ropec", [NL, 64])
    ropes = din("ropes", [NL, 64])
    yc = dout("yc", [512, D])
    yl = dout("yl", [NL, D])
    oak = dout("oak", [2, 2, NCTX, 128])
    oav = dout("oav", [2, 2, NCTX, 128])
    obk = dout("obk", [2, 2, NCTX, 256])
    obv = dout("obv", [2, 2, NCTX, 256])
    ost = dout("ost", [2, 2, 2, 4, 64, 64])
    xls = nc.dram_tensor("xls", [NL, D], F32, kind="Internal").ap()
    catd = nc.dram_tensor("catd", [NL, D], BF16, kind="Internal").ap()
    htd = nc.dram_tensor("htd", [16, 128, D], BF16, kind="Internal").ap()
    hbuf = nc.dram_tensor("hbuf", [NTOK, D], BF16, kind="Internal").ap()
    ysc = nc.dram_tensor("ysc", [NTOK + 128, D], F32, kind="Internal").ap()

    P = Prog()
    stack = contextlib.ExitStack()
    arena_t = stack.enter_context(nc.sbuf_tensor("arena", [128, ARENA_W], F32))
    AR = Arena(arena_t)
    ps = [stack.enter_context(nc.psum_tensor("ps%d" % i, [128, 512], F32)) for i in range(8)]
    psb = [p[:].bitcast(BF16) for p in ps]

    def DMA(eng, out, in_, r=(), w=(), **kw):
        P.op(eng, lambda e: e.dma_start(out=out, in_=in_, **kw), r, w, dma=True)

    def MM(out, lhsT, rhs, st, sp, r, w):
        P.op("tensor", lambda e: e.matmul(out, lhsT=lhsT, rhs=rhs, start=st, stop=sp), r, w)

    def TR(out, in_, ident, r, w):
        P.op("tensor", lambda e: e.transpose(out, in_, ident), r, w)

    def TT(eng, out, in0, in1, op, r, w):
        P.op(eng, lambda e: e.tensor_tensor(out=out, in0=in0, in1=in1, op=op), r, w)

    def TS(eng, out, in0, s1, s2, op0, op1, r, w):
        if op1 is None:
            P.op(eng, lambda e: e.tensor_scalar(out=out, in0=in0, scalar1=s1, scalar2=None, op0=op0), r, w)
        else:
            P.op(eng, lambda e: e.tensor_scalar(out=out, in0=in0, scalar1=s1, scalar2=s2, op0=op0, op1=op1), r, w)

    def STT(eng, out, in0, sc, in1, op0, op1, r, w):
        P.op(eng, lambda e: e.scalar_tensor_tensor(out=out, in0=in0, scalar=sc, in1=in1, op0=op0, op1=op1), r, w)

    def CP(eng, out, in_, r, w):
        if eng == "scalar":
            P.op(eng, lambda e: e.activation(out=out, in_=in_, func=AF.Identity), r, w)
        else:
            P.op(eng, lambda e: e.tensor_copy(out=out, in_=in_), r, w)

    def ACT(out, in_, func, r, w, bias=None, scale=None, accum=None):
        kw = {}
        if bias is not None:
            kw["bias"] = bias
        if scale is not None:
            kw["scale"] = scale
        if accum is not None:
            kw["accum_out"] = accum
        P.op("scalar", lambda e: e.activation(out=out, in_=in_, func=func, **kw), r, w)

    def RED(eng, out, in_, op, r, w):
        P.op(eng, lambda e: e.tensor_reduce(out=out, in_=in_, axis=AX.X, op=op), r, w)

    def MS(eng, ap, val, w):
        P.op(eng, lambda e: e.memset(ap, val), (), w)

    def pk(i):
        return "ps%d" % i

    rot = {"mm": [0, 1], "tr": [2, 3], "sc": [4, 5], "sc4": [4, 5, 0, 1]}
    rotc = {"mm": 0, "tr": 0, "sc": 0, "sc4": 0}

    def bank(kind):
        i = rot[kind][rotc[kind] % len(rot[kind])]
        rotc[kind] += 1
        return i

    CF = AR.f32(NCF)
    CB = AR.bf(640)
    XC = AR.f32(4 * D).rearrange("p (b d) -> p b d", b=4)
    MREG = AR.f32(5 * D)
    LNG = MREG[:, 0:D]
    LNB = MREG[:, D:2 * D]
    M0 = MREG[:, 2 * D:3 * D]
    M1 = MREG[:, 3 * D:4 * D]
    M2 = MREG[:, 4 * D:5 * D]
    SCB = AR.bf(2 * 8 * 128).rearrange("p (c k m) -> p c k m", c=2, k=8)
    SM = AR.f32(160)
    CT = SM[:, 0:16].rearrange("p (c k) -> p c k", c=2)
    ESINK = SM[:, 16:24]
    LGT = SM[:, 24:32]
    EPSB = SM[:, 32:33]
    ZEROB = SM[:, 33:34]
    ONEB = SM[:, 34:35]
    GW = AR.f32(256)
    WRB = AR.bf(8 * 16).rearrange("p (k e) -> p k e", k=8)
    WRF = AR.f32(8 * 16).rearrange("p (k e) -> p k e", k=8)
    AFF = AR.f32(NGB * 16).rearrange("p (b e) -> p b e", b=NGB)
    WST = [AR.f32(8 * 256).rearrange("p (k c) -> p k c", k=8) for _ in range(2)]
    PERSIST_TOP = AR.top
    WSL = []

    def alloc_wsl(n):
        del WSL[:]
        for _ in range(n):
            WSL.append(AR.bf(8 * 512).rearrange("p (k c) -> p k c", k=8))

    IDB = CB[:, C_ID:C_ID + 128]
    IDF = CF[:, C_ID:C_ID + 128]
    TRI1 = CB[:, C_TRI1:C_TRI1 + 128]
    TRI2 = CB[:, C_TRI2:C_TRI2 + 128]
    SUB = CB[:, C_SU:C_SU + 128]
    ONESB = CB[:, C_ONE:C_ONE + 128]

    wctr = [0, 0]

    def wload(src3, cast_eng="scalar"):
        ncols = src3.shape[2]
        j = wctr[1] % len(WSL)
        wctr[1] += 1
        key = "wsl%d" % j
        for kh in range(2):
            s_ = wctr[0] % len(WST)
            wctr[0] += 1
            stv = WST[s_].rearrange("p k c -> p (k c)")[:, 0:4 * ncols].rearrange("p (k c) -> p k c", k=4)
            DMA("sync", stv, src3[:, kh * 4:(kh + 1) * 4, :], w=["wst%d" % s_])
            CP(cast_eng, WSL[j][:, kh * 4:(kh + 1) * 4, 0:ncols], stv, r=["wst%d" % s_], w=[key + "_%d" % kh])
        return WSL[j][:, :, 0:ncols], [key + "_0", key + "_1"]

    def wsrc(w2d, c0, c1):
        return w2d.rearrange("(k p) c -> p k c", p=128)[:, :, c0:c1]

    DMA("sync", CF, cfd, w=["CF"])
    CP("vector", CB, CF[:, 0:640], r=["CF"], w=["CB"])
    DMA("sync", XC, xc.rearrange("(b p) d -> p b d", p=128), w=["XC0", "XC1", "XC2", "XC3"])
    DMA("sync", CT, cond.rearrange("c (k p) -> p c k", p=128), w=["CT"], allow_slow_non_contiguous=True)
    MS("vector", EPSB, EPS, ["EPSB"])
    MS("vector", ZEROB, 0.0, ["ZEROB"])
    MS("vector", ONEB, 1.0, ["ONEB"])
    ACT(CT, CT, AF.Silu, r=["CT"], w=["CT"])
    for c in range(2):
        CP("vector", SCB[:, c], CT[:, c, :].unsqueeze(2).to_broadcast([128, 8, 128]), r=["CT"], w=["SCB"])

    def get_mod(dst, dkey, c, l, part, add_one):
        DMA("sync", dst, b_ada[l, part * D:(part + 1) * D].partition_broadcast(128), w=[dkey])
        for ct in range(2):
            sl, sk = wload(wsrc(w_ada[l], part * D + ct * 512, part * D + ct * 512 + 512))
            b = bank("mm")
            for k in range(8):
                MM(ps[b][:, :], SCB[:, c, k, :], sl[:, k, :], k == 0, k == 7, r=sk + ["SCB"], w=[pk(b)])
            STT("vector", dst[:, ct * 512:(ct + 1) * 512], ps[b][:, :], 1.0 if add_one else 0.0,
                dst[:, ct * 512:(ct + 1) * 512], ALU.add, ALU.add, r=[pk(b)], w=[pk(b), dkey])

    class Stream:
        pass

    SC_ = Stream()
    SC_.lat = False
    SC_.nblk = 4
    SC_.cond = 0
    SC_.gb0 = 0
    SL_ = Stream()
    SL_.lat = True
    SL_.nblk = 16
    SL_.cond = 1
    SL_.gb0 = 4

    def ln_epilogue(T1, xin_ap, xin_keys, gmod, gkey, out_ap, out_keys, tmp, tkey, mv_tiles):
        stats, mv, rstd, nmr = mv_tiles
        TT("vector", T1, T1, gmod, ALU.mult, r=[tkey, gkey], w=[tkey])
        STT("vector", T1, xin_ap, ALPHA, T1, ALU.mult, ALU.add, r=[tkey] + xin_keys, w=[tkey])
        P.op("vector", lambda e: e.bn_stats(out=stats[:, 0:6], in_=T1[:, 0:512]), [tkey], ["lnst"])
        P.op("vector", lambda e: e.bn_stats(out=stats[:, 6:12], in_=T1[:, 512:1024]), [tkey], ["lnst2"])
        P.op("vector", lambda e: e.bn_aggr(out=mv, in_=stats), ["lnst", "lnst2"], ["lnmv"])
        ACT(rstd, mv[:, 1:2], AF.Ln, r=["lnmv", "EPSB"], w=["lnrs"], bias=EPSB)
        ACT(rstd, rstd, AF.Exp, r=["lnrs"], w=["lnrs"], scale=-0.5)
        STT("vector", nmr, mv[:, 0:1], -1.0, rstd, ALU.mult, ALU.mult, r=["lnmv", "lnrs"], w=["lnnm"])
        ACT(tmp, T1, AF.Identity, r=[tkey, "lnrs", "lnnm"], w=[tkey + "n"], bias=nmr, scale=rstd)
        TT("gpsimd", tmp, tmp, LNG, ALU.mult, r=[tkey + "n", "LNG"], w=[tkey + "n"])
        TT("gpsimd", out_ap, tmp, LNB, ALU.add, r=[tkey + "n", "LNB"], w=out_keys)


    hctr = [0]
    octr = [0]

    def obank():
        i = 6 + (octr[0] % 2)
        octr[0] += 1
        return i

    def mixer(S, l):
        nblk = S.nblk
        lat = S.lat
        NT = nblk * 128
        nreq = 1 if lat else 2
        bpr = nblk // nreq
        xsrc = (xl if l == 0 else xls) if lat else None
        alloc_wsl(3)
        del WST[2:]
        for _ in range(1 if lat else 2):
            WST.append(AR.f32(8 * 256).rearrange("p (k c) -> p k c", k=8))
        get_mod(M0, "M0", S.cond, l, 1, True)
        get_mod(M1, "M1", S.cond, l, 0, False)
        XS = [AR.f32(D), AR.f32(D)]
        HF = AR.f32(D)
        HB = [AR.bf(D), AR.bf(D)]
        HT = [AR.bf(8 * 128).rearrange("p (k t) -> p k t", k=8) for _ in range(2)]
        STG = [AR.bf(512), AR.bf(512)]
        CS = [AR.bf(4 * 512).rearrange("p (s c) -> p s c", s=4) for _ in range(2)]
        PTH = [None, None]
        SMX = AR.f32(64)
        DEN = SMX[:, 0:4]
        RDEN = SMX[:, 4:8]
        KO = [AR.f32(256), AR.f32(256)] if not lat else None
        sctr = [0]
        cctr = [0]
        kctr = [0]
        GTOP = AR.top

        def xblock(b):
            if lat:
                i = hctr[0] % 2
                hctr[0] += 1
                DMA("sync", XS[i], xsrc[b * 128:(b + 1) * 128, :], r=["xls"], w=["XS%d" % i])
                return XS[i], ["XS%d" % i]
            return XC[:, b, :], ["XC%d" % b]

        def to_hT(hb, hbk):
            i = cctr[0] % 2
            cctr[0] += 1
            tb_ = bank("tr")
            for k in range(8):
                TR(psb[tb_][:, k * 128:(k + 1) * 128], hb[:, k * 128:(k + 1) * 128], IDB, r=[hbk, "CB"], w=[pk(tb_)])
            CP("scalar", HT[i], psb[tb_][:, :].rearrange("p (k t) -> p k t", k=8), r=[pk(tb_)], w=[pk(tb_), "HT%d" % i])
            return HT[i], "HT%d" % i

        hmode = ["compute"]

        def hblock(b):
            if hmode[0] == "load":
                i = cctr[0] % 2
                cctr[0] += 1
                DMA("sync", HT[i].rearrange("p k t -> p (k t)"), htd[b], r=["htd"], w=["HT%d" % i])
                return HT[i], "HT%d" % i
            xap, xk = xblock(b)
            i = kctr[0] % 2
            kctr[0] += 1
            TT("vector", HF, xap, M0, ALU.mult, r=xk + ["M0"], w=["HF"])
            TT("vector", HB[i], HF, M1, ALU.add, r=["HF", "M1"], w=["HB%d" % i])
            ht, htk = to_hT(HB[i], "HB%d" % i)
            DMA("gpsimd", htd[b], ht.rearrange("p k t -> p (k t)"), r=[htk], w=["htd"])
            return ht, htk

        def proj(ht, htk, sl, sk, ncols):
            b = bank("mm")
            for k in range(8):
                MM(ps[b][:, 0:ncols], ht[:, k, :], sl[:, k, 0:ncols], k == 0, k == 7, r=[htk] + sk, w=[pk(b)])
            return b

        def stage():
            i = sctr[0] % 2
            sctr[0] += 1
            return STG[i], "STG%d" % i

        def attn_scores(job, par):
            qT_fn, qkeys, nsub, blocks, esink_ap, cs_dst, cs_key, post = job
            PT = PTH[par]
            for bi, blk in enumerate(blocks):
                sb_ = bank("sc4")
                w = (blk["hi"] - blk["lo"]) * 128
                ptk = "PT%d_%d" % (par, bi)
                MM(ps[sb_][:, 0:w], blk["kT"], qT_fn(blk["lo"], blk["hi"], blk["K"]), True, True,
                   r=blk["r"] + qkeys, w=[pk(sb_)])
                ACT(PT[:, bi, 0:w], ps[sb_][:, 0:w], AF.Exp, r=[pk(sb_)], w=[pk(sb_), ptk], scale=0.125)
                for (c0, c1, m, mk) in blk["masks"]:
                    TT("vector" if (c1 - c0) > 128 else "gpsimd", PT[:, bi, c0:c1], PT[:, bi, c0:c1], m, ALU.mult, r=[ptk] + mk, w=[ptk])

        def attn_pv(job, par):
            qT_fn, qkeys, nsub, blocks, esink_ap, cs_dst, cs_key, post = job
            PT = PTH[par]
            ob = obank()
            for s_ in range(nsub):
                rel = [(bi, blk) for bi, blk in enumerate(blocks) if blk["lo"] <= s_ < blk["hi"]]
                for n_, (bi, blk) in enumerate(rel):
                    o0 = (s_ - blk["lo"]) * 128
                    MM(ps[ob][:, s_ * 65:(s_ + 1) * 65], PT[:, bi, o0:o0 + 128], blk["va"], n_ == 0, n_ == len(rel) - 1,
                       r=["PT%d_%d" % (par, bi)] + blk["r"], w=[pk(ob)])
            O = ps[ob][:, 0:nsub * 65].rearrange("p (s d) -> p s d", d=65)
            TS("vector", DEN[:, 0:nsub], O[:, :, 64], esink_ap if esink_ap is not None else 0.0, None, ALU.add, None,
               r=[pk(ob), "ESINK"], w=[pk(ob), "DEN"])
            P.op("vector", lambda e: e.reciprocal(out=RDEN[:, 0:nsub], in_=DEN[:, 0:nsub]), ["DEN"], ["RDEN"])
            TT("vector", cs_dst, O[:, :, 0:64], RDEN[:, 0:nsub].unsqueeze(2).to_broadcast([128, nsub, 64]), ALU.mult,
               r=[pk(ob), "RDEN"], w=[pk(ob), cs_key])
            if post is not None:
                post()

        def run_attn(jobs, mid_hook=None):
            if not jobs:
                return
            attn_scores(jobs[0], 0)
            for i, job in enumerate(jobs):
                if i + 1 < len(jobs):
                    attn_scores(jobs[i + 1], (i + 1) % 2)
                attn_pv(job, i % 2)
                if mid_hook is not None and i == len(jobs) // 3:
                    mid_hook()

        def cs_to_catT(cs, cs_key, sub0, nsub, nch, gc0):
            dst = catd[sub0 * 128:(sub0 + nsub) * 128, gc0 * 128:(gc0 + nch) * 128].rearrange("(s p) c -> p s c", p=128)
            DMA("gpsimd", dst, cs[:, 0:nsub, 0:nch * 128], r=[cs_key], w=["catd"])

        def rope(src, nh, b, dst, dkey, srckey, RC, RS, R1, R2):
            x3 = src.rearrange("p (h d) -> p h d", h=nh)
            x5 = src.rearrange("p (h r f x) -> p h r f x", h=nh, r=2, f=2)
            r5 = R2[:, 0:nh * 64].rearrange("p (h r f x) -> p h r f x", h=nh, r=2, f=2)
            c3 = RC[:, b, :].unsqueeze(1).to_broadcast([128, nh, 64])
            s4 = RS[:, b, :].rearrange("p (r f x) -> p r f x", r=2, f=2)
            TT("vector", R1[:, 0:nh * 64].rearrange("p (h d) -> p h d", h=nh), x3, c3, ALU.mult, r=[srckey, "ROPE"], w=[srckey, "R1"])
            TT("vector", r5[:, :, :, 0, :], x5[:, :, :, 1, :], s4[:, :, 0, :].unsqueeze(1).to_broadcast([128, nh, 2, 16]), ALU.mult,
               r=[srckey, "ROPE"], w=[srckey, "R2a"])
            TT("vector", r5[:, :, :, 1, :], x5[:, :, :, 0, :], s4[:, :, 1, :].unsqueeze(1).to_broadcast([128, nh, 2, 16]), ALU.mult,
               r=[srckey, "ROPE"], w=[srckey, "R2b"])
            TT("vector", dst, R1[:, 0:nh * 64], R2[:, 0:nh * 64], ALU.add, r=["R1", "R2a", "R2b"], w=[dkey])

        def group_A():
            QT = AR.bf(4 * NT).rearrange("p (q t) -> p q t", q=4)
            KT = AR.bf(2 * NT).rearrange("p (q t) -> p q t", q=2)
            VA = AR.bf(nblk * 2 * 65).rearrange("p (b h d) -> p b h d", b=nblk, h=2)
            PTH[0] = AR.bf(12 * 512).rearrange("p (b c) -> p b c", b=12)
            PTH[1] = AR.bf(12 * 512).rearrange("p (b c) -> p b c", b=12)
            MS("gpsimd", VA, 1.0, ["VA"])
            if lat:
                KCT = AR.bf(2 * 512).rearrange("p (q t) -> p q t", q=2)
                VCA = AR.bf(4 * 2 * 65).rearrange("p (b h d) -> p b h d", b=4, h=2)
                CST = AR.f32(4 * 128).rearrange("p (b c) -> p b c", b=4)
                CSB = AR.bf(4 * 256).rearrange("p (b c) -> p b c", b=4)
                RC = AR.f32(nblk * 64).rearrange("p (b d) -> p b d", b=nblk)
                RS = AR.f32(nblk * 64).rearrange("p (b d) -> p b d", b=nblk)
                R1 = AR.f32(512)
                R2 = AR.f32(512)
                DMA("sync", RC, ropec.rearrange("(b p) d -> p b d", p=128), w=["ROPE"])
                DMA("sync", RS, ropes.rearrange("(b p) d -> p b d", p=128), w=["ROPE"])
                MS("gpsimd", VCA, 1.0, ["VCA"])
                DMA("sync", CST, cak[l].rearrange("(b p) c -> p b c", p=128), w=["CST"])
                CP("vector", CSB.rearrange("p b (h u d) -> p b h u d", h=2, u=2)[:, :, :, 0, :],
                   CST.rearrange("p b (h d) -> p b h d", h=2), r=["CST"], w=["CSBa"])
                CP("vector", CSB.rearrange("p b (h u d) -> p b h u d", h=2, u=2)[:, :, :, 1, :],
                   CST.rearrange("p b (h d) -> p b h d", h=2), r=["CST"], w=["CSBb"])
                for kv in range(2):
                    tb_ = bank("tr")
                    for cb in range(4):
                        TR(psb[tb_][:, cb * 128:(cb + 1) * 128], CSB[:, cb, kv * 128:(kv + 1) * 128], IDB, r=["CSBa", "CSBb", "CB"], w=[pk(tb_)])
                    CP("vector", KCT[:, kv, :], psb[tb_][:, 0:512], r=[pk(tb_)], w=[pk(tb_), "KCT"])
                DMA("sync", CST, cav[l].rearrange("(b p) c -> p b c", p=128), r=["CSBa", "CSBb"], w=["CST"])
                CP("vector", VCA[:, :, :, 0:64], CST.rearrange("p b (h d) -> p b h d", h=2), r=["CST"], w=["VCA"])
            w0, k0 = wload(wsrc(w_in[l], 0, 512))
            w1, k1 = wload(wsrc(w_in[l], 512, 896))
            nxt = hblock(0)
            for b in range(nblk):
                ht, htk = nxt
                if b + 1 < nblk:
                    nxt = hblock(b + 1)
                b0 = proj(ht, htk, w0, k0, 512)
                sg, sgk = stage()
                if lat:
                    rope(ps[b0][:, 0:512], 8, b, sg, sgk, pk(b0), RC, RS, R1, R2)
                else:
                    CP("scalar", sg, ps[b0][:, 0:512], r=[pk(b0)], w=[pk(b0), sgk])
                tb_ = bank("tr")
                for q in range(4):
                    TR(psb[tb_][:, q * 128:(q + 1) * 128], sg[:, q * 128:(q + 1) * 128], IDB, r=[sgk, "CB"], w=[pk(tb_)])
                CP("vector", QT[:, :, b * 128:(b + 1) * 128], psb[tb_][:, 0:512].rearrange("p (q t) -> p q t", q=4),
                   r=[pk(tb_)], w=[pk(tb_), "QT"])
                b1 = proj(ht, htk, w1, k1, 384)
                sg, sgk = stage()
                if lat:
                    rope(ps[b1][:, 0:256], 4, b, sg[:, 0:256], sgk, pk(b1), RC, RS, R1, R2)
                else:
                    CP("scalar", sg[:, 0:256], ps[b1][:, 0:256], r=[pk(b1)], w=[pk(b1), sgk])
                tb_ = bank("tr")
                for q in range(2):
                    TR(psb[tb_][:, q * 128:(q + 1) * 128], sg[:, q * 128:(q + 1) * 128], IDB, r=[sgk, "CB"], w=[pk(tb_)])
                CP("vector", KT[:, :, b * 128:(b + 1) * 128], psb[tb_][:, 0:256].rearrange("p (q t) -> p q t", q=2),
                   r=[pk(tb_)], w=[pk(tb_), "KT"])
                CP("scalar", VA[:, b, :, 0:64], ps[b1][:, 256:384].rearrange("p (h d) -> p h d", h=2), r=[pk(b1)], w=[pk(b1), "VA"])
                if not lat:
                    rq, tb0 = divmod(b, 2)
                    ko = KO[b % 2]
                    kok = "KO%d" % (b % 2)
                    CP("vector", ko[:, 0:128].rearrange("p (h d) -> p h d", h=2),
                       ps[b1][:, 0:256].rearrange("p (h u d) -> p h u d", h=2, u=2)[:, :, 0, :], r=[pk(b1)], w=[pk(b1), kok])
                    CP("vector", ko[:, 128:256], ps[b1][:, 256:384], r=[pk(b1)], w=[pk(b1), kok])
                    DMA("gpsimd", oak[rq, l, tb0 * 128:(tb0 + 1) * 128, :], ko[:, 0:128], r=[kok])
                    DMA("gpsimd", oav[rq, l, tb0 * 128:(tb0 + 1) * 128, :], ko[:, 128:256], r=[kok])
            ntile = nblk // 4 if lat else nreq
            nsub = 4 if lat else 2
            jobs = []
            for t in range(ntile):
                cs = CS[t % 2]
                csk = "CS%d" % (t % 2)
                for h in range(8):
                    p_, a_ = divmod(h, 2)
                    kv = h // 4
                    rows = slice(a_ * 64, (a_ + 1) * 64)
                    blocks = []
                    if lat:
                        for cb in range(4):
                            blocks.append(dict(kT=KCT[rows, kv, cb * 128:(cb + 1) * 128], va=VCA[:, cb, kv, :], lo=0, hi=4,
                                               K=64, masks=[], r=["KCT", "VCA"]))
                        for j in range(max(0, 4 * t - 1), min(15, 4 * t + 4) + 1):
                            lo = max(j - 1, 4 * t) - 4 * t
                            hi = min(j + 1, 4 * t + 3) - 4 * t + 1
                            masks = []
                            for i in range(lo + 4 * t, hi + 4 * t):
                                c0 = (i - 4 * t - lo) * 128
                                if i == j + 1:
                                    masks.append((c0, c0 + 128, TRI1, ["CB"]))
                                elif i == j - 1:
                                    masks.append((c0, c0 + 128, TRI2, ["CB"]))
                            blocks.append(dict(kT=KT[rows, kv, j * 128:(j + 1) * 128], va=VA[:, j, kv, :], lo=lo, hi=hi,
                                               K=64, masks=masks, r=["KT", "VA"]))
                    else:
                        for j in range(2 * t, 2 * t + 2):
                            blocks.append(dict(kT=KT[rows, kv, j * 128:(j + 1) * 128], va=VA[:, j, kv, :], lo=0, hi=2,
                                               K=64, masks=[], r=["KT", "VA"]))
                    q0 = t * nsub

                    def qfn(lo, hi, K, p_=p_, rows=rows, q0=q0):
                        return QT[rows, p_, (q0 + lo) * 128:(q0 + hi) * 128]

                    post = None
                    if h == 7:
                        post = (lambda cs=cs, csk=csk, t=t: cs_to_catT(cs, csk, t * nsub, nsub, 4, 0))
                    jobs.append((qfn, ["QT"], nsub, blocks, ESINK[:, h:h + 1], cs[:, 0:nsub, h * 64:(h + 1) * 64], csk, post))

            def later_mods():
                get_mod(M2, "M2", S.cond, l, 2, False)
                get_mod(M0, "M0", S.cond, l, 4, True)
                get_mod(M1, "M1", S.cond, l, 3, False)

            run_attn(jobs, later_mods)

        def group_B(hp):
            QTB = AR.bf(2 * NT).rearrange("p (q t) -> p q t", q=2)
            KTB = AR.bf(2 * NT).rearrange("p (q t) -> p q t", q=2)
            VB = AR.bf(nblk * 2 * 65).rearrange("p (b h d) -> p b h d", b=nblk, h=2)
            PTH[0] = AR.bf(12 * 512).rearrange("p (b c) -> p b c", b=12)
            PTH[1] = AR.bf(12 * 512).rearrange("p (b c) -> p b c", b=12)
            MS("gpsimd", VB, 1.0, ["VB"])
            if lat:
                T2 = AR.bf(2 * 22 * 64).rearrange("p (h c) -> p h c", h=2)
                T2F = AR.f32(22 * 64)
                KCTB = AR.bf(2 * 512).rearrange("p (q t) -> p q t", q=2)
                VCB = AR.bf(4 * 2 * 65).rearrange("p (b h d) -> p b h d", b=4, h=2)
                CST = AR.f32(4 * 128).rearrange("p (b c) -> p b c", b=4)
                CSB = AR.bf(4 * 128).rearrange("p (b c) -> p b c", b=4)
                MS("gpsimd", VCB, 1.0, ["VCB"])
                for hh in range(2):
                    DMA("sync", T2F, tbd[l, 2 * hp + hh], w=["T2F"])
                    ACT(T2[:, hh, :], T2F, AF.Exp, r=["T2F"], w=["T2"])
                    for half in range(2):
                        cs_ = slice(half * 1024, (half + 1) * 1024)
                        DMA("gpsimd", KTB[64:96, hh, cs_], augk[:, cs_], w=["KTBaug"])
                        DMA("gpsimd", QTB[64:96, hh, cs_], augq[:, cs_], w=["QTBaug"])
                DMA("sync", CST, cbk[l].rearrange("(b p) c -> p b c", p=128)[:, :, hp * 128:(hp + 1) * 128], w=["CST"])
                CP("vector", CSB, CST, r=["CST"], w=["CSB"])
                for hh in range(2):
                    tb_ = bank("tr")
                    for cb in range(4):
                        TR(psb[tb_][0:64, cb * 128:(cb + 1) * 128], CSB[:, cb, hh * 64:(hh + 1) * 64], IDB, r=["CSB", "CB"], w=[pk(tb_)])
                    CP("vector", KCTB[0:64, hh, :], psb[tb_][0:64, 0:512], r=[pk(tb_)], w=[pk(tb_), "KCTB"])
                DMA("sync", CST, cbv[l].rearrange("(b p) c -> p b c", p=128)[:, :, hp * 128:(hp + 1) * 128], r=["CSB"], w=["CST"])
                CP("vector", VCB[:, :, :, 0:64], CST.rearrange("p b (h d) -> p b h d", h=2), r=["CST"], w=["VCB"])
            c0 = 896 + hp * 384
            w0, k0 = wload(wsrc(w_in[l], c0, c0 + 384))
            nxt = hblock(0)
            for b in range(nblk):
                ht, htk = nxt
                if b + 1 < nblk:
                    nxt = hblock(b + 1)
                b0 = proj(ht, htk, w0, k0, 384)
                sg, sgk = stage()
                CP("scalar", sg[:, 0:256], ps[b0][:, 0:256], r=[pk(b0)], w=[pk(b0), sgk])
                tb_ = bank("tr")
                for q in range(4):
                    TR(psb[tb_][0:64, q * 128:(q + 1) * 128], sg[:, q * 64:(q + 1) * 64], IDB, r=[sgk, "CB"], w=[pk(tb_)])
                CP("vector", QTB[0:64, :, b * 128:(b + 1) * 128], psb[tb_][0:64, 0:256].rearrange("p (q t) -> p q t", q=2),
                   r=[pk(tb_)], w=[pk(tb_), "QTB"])
                CP("vector", KTB[0:64, :, b * 128:(b + 1) * 128], psb[tb_][0:64, 256:512].rearrange("p (q t) -> p q t", q=2),
                   r=[pk(tb_)], w=[pk(tb_), "KTB"])
                CP("scalar", VB[:, b, :, 0:64], ps[b0][:, 256:384].rearrange("p (h d) -> p h d", h=2), r=[pk(b0)], w=[pk(b0), "VB"])
                if not lat:
                    rq, tb0 = divmod(b, 2)
                    ko = KO[b % 2]
                    kok = "KO%d" % (b % 2)
                    CP("vector", ko[:, 0:256], ps[b0][:, 128:384], r=[pk(b0)], w=[pk(b0), kok])
                    DMA("gpsimd", obk[rq, l, tb0 * 128:(tb0 + 1) * 128, hp * 128:(hp + 1) * 128], ko[:, 0:128], r=[kok])
                    DMA("gpsimd", obv[rq, l, tb0 * 128:(tb0 + 1) * 128, hp * 128:(hp + 1) * 128], ko[:, 128:256], r=[kok])
            ntile = nblk // 4 if lat else nreq
            nsub = 4 if lat else 2
            jobs = []
            for t in range(ntile):
                cs = CS[t % 2]
                csk = "CS%d" % (t % 2)
                for hh in range(2):
                    blocks = []
                    if lat:
                        for cb in range(4):
                            blocks.append(dict(kT=KCTB[0:64, hh, cb * 128:(cb + 1) * 128], va=VCB[:, cb, hh, :], lo=0, hi=4,
                                               K=64, masks=[], r=["KCTB", "VCB"]))
                        for j in range(max(0, 4 * t - 2), min(15, 4 * t + 5) + 1):
                            m0 = 2 * (4 * t - j) + 7
                            mk = T2[:, hh, (m0 + 3) * 64:(m0 + 3 + 8) * 64]
                            blocks.append(dict(kT=KTB[0:96, hh, j * 128:(j + 1) * 128], va=VB[:, j, hh, :], lo=0, hi=4,
                                               K=96, masks=[(0, 512, mk, ["T2"])], r=["KTB", "KTBaug", "VB"]))
                    else:
                        for j in range(2 * t, 2 * t + 2):
                            blocks.append(dict(kT=KTB[0:64, hh, j * 128:(j + 1) * 128], va=VB[:, j, hh, :], lo=0, hi=2,
                                               K=64, masks=[], r=["KTB", "VB"]))
                    q0 = t * nsub

                    def qfn(lo, hi, K, hh=hh, q0=q0):
                        return QTB[0:K, hh, (q0 + lo) * 128:(q0 + hi) * 128]

                    post = None
                    if hh == 1:
                        post = (lambda cs=cs, csk=csk, t=t: cs_to_catT(cs, csk, t * nsub, nsub, 1, 4 + hp))
                    jobs.append((qfn, ["QTB", "QTBaug"], nsub, blocks, None, cs[:, 0:nsub, hh * 64:(hh + 1) * 64], csk, post))
            run_attn(jobs)

        def group_C():
            QTC = AR.bf(2 * NT).rearrange("p (q t) -> p q t", q=2)
            KTC = AR.bf(2 * NT).rearrange("p (q t) -> p q t", q=2)
            KTOK = AR.bf(nblk * 256).rearrange("p (b c) -> p b c", b=nblk)
            VTOK = AR.bf(nblk * 256).rearrange("p (b c) -> p b c", b=nblk)
            GS = AR.bf(nblk * 256).rearrange("p (b c) -> p b c", b=nblk)
            DTT = AR.bf(8 * 128).rearrange("p (h i) -> p h i", h=8)
            DTF = AR.f32(128)
            XIP = AR.f32(4 * 128).rearrange("p (d q i) -> p d q i", d=2, q=2)
            ZT = AR.f32(8)
            GCHF = AR.f32(8)
            GCHP = AR.f32(4).rearrange("p (d q) -> p d q", d=2)
            SACC = AR.f32(4 * 128).rearrange("p (d q e) -> p d q e", d=2, q=2)
            SBF = AR.bf(2 * bpr * 2 * 64).rearrange("p (d c q e) -> p d c q e", d=2, c=bpr, q=2)
            KZ = [AR.bf(256), AR.bf(256)]
            PD = [AR.bf(8 * 128).rearrange("p (d h i) -> p d h i", d=2, h=4) for _ in range(2)]
            QX = [AR.bf(4 * 128).rearrange("p (d q i) -> p d q i", d=2, q=2) for _ in range(2)]
            O32 = AR.f32(256)
            SQ = AR.f32(256)
            GST = AR.f32(32)
            DMA("sync", LGT, dec[l, :].partition_broadcast(128), w=["LGT"])
            ACT(LGT, LGT, AF.Exp, r=["LGT"], w=["LGT"])
            TS("vector", LGT, LGT, -1.0, None, ALU.mult, None, r=["LGT"], w=["LGT"])
            for dh in range(8):
                d_ = dh // 4
                ACT(DTF, CF[:, (C_RELP if d_ == 0 else C_RELN):(C_RELP if d_ == 0 else C_RELN) + 128], AF.Exp,
                    r=["CF", "LGT"], w=["DTF"], scale=LGT[:, dh:dh + 1])
                TT("vector", DTT[:, dh, :], DTF, CF[:, (C_MF if d_ == 0 else C_MB):(C_MF if d_ == 0 else C_MB) + 128], ALU.mult,
                   r=["DTF", "CF"], w=["DTT"])
                zc = C_ZF if d_ == 0 else C_ZB
                ACT(ZT[:, dh:dh + 1], CF[:, zc:zc + 1], AF.Exp, r=["CF", "LGT"], w=["ZT"], scale=LGT[:, dh:dh + 1])
                ACT(GCHF[:, dh:dh + 1], LGT[:, dh:dh + 1], AF.Exp, r=["LGT"], w=["GCHF"], scale=128.0)
            TS("vector", ZT, ZT, 0.125, None, ALU.mult, None, r=["ZT"], w=["ZT"])
            for d_ in range(2):
                for q in range(2):
                    for a_ in range(2):
                        rows = slice(a_ * 64, (a_ + 1) * 64)
                        dh = d_ * 4 + 2 * q + a_
                        xc0 = C_XIF if d_ == 0 else C_XIB
                        ACT(XIP[rows, d_, q, :], CF[rows, xc0:xc0 + 128], AF.Exp, r=["CF", "LGT"], w=["XIP"], scale=LGT[rows, dh:dh + 1])
                        CP("vector", GCHP[rows, d_, q:q + 1], GCHF[rows, dh:dh + 1], r=["GCHF"], w=["GCHP"])
            w0, k0 = wload(wsrc(w_in[l], 1664, 2176))
            w1, k1 = wload(wsrc(w_in[l], 2176, 2688))
            nxt = hblock(0)
            for b in range(nblk):
                ht, htk = nxt
                if b + 1 < nblk:
                    nxt = hblock(b + 1)
                b0 = proj(ht, htk, w0, k0, 512)
                sg, sgk = stage()
                CP("scalar", sg, ps[b0][:, 0:512], r=[pk(b0)], w=[pk(b0), sgk])
                tb_ = bank("tr")
                for q in range(4):
                    TR(psb[tb_][:, q * 128:(q + 1) * 128], sg[:, q * 128:(q + 1) * 128], IDB, r=[sgk, "CB"], w=[pk(tb_)])
                CP("vector", QTC[:, :, b * 128:(b + 1) * 128], psb[tb_][:, 0:256].rearrange("p (q t) -> p q t", q=2),
                   r=[pk(tb_)], w=[pk(tb_), "QTC"])
                CP("vector", KTC[:, :, b * 128:(b + 1) * 128], psb[tb_][:, 256:512].rearrange("p (q t) -> p q t", q=2),
                   r=[pk(tb_)], w=[pk(tb_), "KTC"])
                CP("gpsimd", KTOK[:, b, :], sg[:, 256:512], r=[sgk], w=["KTOK"])
                b1 = proj(ht, htk, w1, k1, 512)
                CP("vector", VTOK[:, b, :], ps[b1][:, 0:256], r=[pk(b1)], w=[pk(b1), "VTOK"])
                ACT(GS[:, b, :], ps[b1][:, 256:512], AF.Silu, r=[pk(b1)], w=[pk(b1), "GS"])
            for rq in range(nreq):
                gb0 = rq * bpr
                MS("vector", SACC, 0.0, ["SACC0", "SACC1"])
                if lat:
                    for d_ in range(2):
                        for h in range(4):
                            q, a_ = divmod(h, 2)
                            rows = slice(a_ * 64, (a_ + 1) * 64)
                            DMA("sync", SACC[rows, d_, q, a_ * 64:(a_ + 1) * 64], st0[l, d_, h], w=["SACC%d" % d_])
                for step in range(bpr):
                    for d_ in range(2):
                        c = step if d_ == 0 else bpr - 1 - step
                        gb = gb0 + c
                        sk_ = "SACC%d" % d_
                        for a_ in range(2):
                            rows = slice(a_ * 64, (a_ + 1) * 64)
                            CP("scalar", SBF[rows, d_, c, :, :], SACC[rows, d_, :, a_ * 64:(a_ + 1) * 64], r=[sk_], w=["SBF"])
                        kz = KZ[(step * 2 + d_) % 2]
                        kzk = "KZ%d" % ((step * 2 + d_) % 2)
                        TT("vector", kz.rearrange("p (h d) -> p h d", h=4), KTOK[:, gb, :].rearrange("p (h d) -> p h d", h=4),
                           ZT[:, d_ * 4:(d_ + 1) * 4].unsqueeze(2).to_broadcast([128, 4, 64]), ALU.mult, r=["KTOK", "ZT"], w=[kzk])
                        sb_ = bank("sc")
                        for q in range(2):
                            MM(ps[sb_][:, q * 128:(q + 1) * 128], kz[:, q * 128:(q + 1) * 128], VTOK[:, gb, q * 128:(q + 1) * 128],
                               True, True, r=[kzk, "VTOK"], w=[pk(sb_)])
                        for q in range(2):
                            STT("vector", SACC[:, d_, q, :], SACC[:, d_, q, :], GCHP[:, d_, q:q + 1], ps[sb_][:, q * 128:(q + 1) * 128],
                                ALU.mult, ALU.add, r=[pk(sb_), "GCHP", sk_], w=[pk(sb_), sk_])
                if not lat:
                    for d_ in range(2):
                        for h in range(4):
                            q, a_ = divmod(h, 2)
                            rows = slice(a_ * 64, (a_ + 1) * 64)
                            DMA("gpsimd", ost[rq, l, d_, h], SACC[rows, d_, q, a_ * 64:(a_ + 1) * 64], r=["SACC%d" % d_])
                for c in range(bpr):
                    gb = gb0 + c
                    bc = slice(gb * 128, (gb + 1) * 128)
                    i2 = c % 2
                    sbs = [bank("sc"), bank("sc")]
                    for a_ in range(2):
                        rows = slice(a_ * 64, (a_ + 1) * 64)
                        for q in range(2):
                            MM(ps[sbs[a_]][:, q * 128:(q + 1) * 128], KTC[rows, q, bc], QTC[rows, q, bc], True, True,
                               r=["KTC", "QTC"], w=[pk(sbs[a_])])
                    for d_ in range(2):
                        for a_ in range(2):
                            TT("vector", PD[i2][:, d_, a_:4:2, :], ps[sbs[a_]][:, 0:256].rearrange("p (q i) -> p q i", q=2),
                               DTT[:, d_ * 4 + a_:d_ * 4 + 4:2, :], ALU.mult, r=[pk(sbs[a_]), "DTT"], w=[pk(sbs[a_]), "PD%d" % i2])
                        TT("gpsimd", QX[i2][:, d_], QTC[:, :, bc], XIP[:, d_], ALU.mult, r=["QTC", "XIP"], w=["QX%d" % i2])
                    ob = obank()
                    for h in range(4):
                        q, a_ = divmod(h, 2)
                        rows = slice(a_ * 64, (a_ + 1) * 64)
                        oc = ps[ob][:, h * 64:(h + 1) * 64]
                        rk = ["PD%d" % i2, "QX%d" % i2, "VTOK", "SBF"]
                        MM(oc, PD[i2][:, 0, h, :], VTOK[:, gb, h * 64:(h + 1) * 64], True, False, r=rk, w=[pk(ob)])
                        MM(oc, QX[i2][rows, 0, q, :], SBF[rows, 0, c, q, :], False, False, r=rk, w=[pk(ob)])
                        MM(oc, PD[i2][:, 1, h, :], VTOK[:, gb, h * 64:(h + 1) * 64], False, False, r=rk, w=[pk(ob)])
                        MM(oc, QX[i2][rows, 1, q, :], SBF[rows, 1, c, q, :], False, True, r=rk, w=[pk(ob)])
                    o3 = O32.rearrange("p (h d) -> p h d", h=4)
                    CP("scalar", O32, ps[ob][:, 0:256], r=[pk(ob)], w=[pk(ob), "O32"])
                    RED("vector", GST[:, 0:4], o3, ALU.add, r=["O32"], w=["GSa"])
                    TT("vector", SQ, O32, O32, ALU.mult, r=["O32"], w=["SQ"])
                    RED("vector", GST[:, 4:8], SQ.rearrange("p (h d) -> p h d", h=4), ALU.add, r=["SQ"], w=["GSb"])
                    TS("vector", GST[:, 0:4], GST[:, 0:4], 1.0 / 64, None, ALU.mult, None, r=["GSa"], w=["GSa"])
                    TT("vector", GST[:, 8:12], GST[:, 0:4], GST[:, 0:4], ALU.mult, r=["GSa"], w=["GSc"])
                    STT("vector", GST[:, 4:8], GST[:, 4:8], 1.0 / 64, GST[:, 8:12], ALU.mult, ALU.subtract, r=["GSb", "GSc"], w=["GSb"])
                    ACT(GST[:, 4:8], GST[:, 4:8], AF.Ln, r=["GSb", "EPSB"], w=["GSb"], bias=EPSB)
                    ACT(GST[:, 4:8], GST[:, 4:8], AF.Exp, r=["GSb"], w=["GSb"], scale=-0.5)
                    TT("vector", o3, o3, GST[:, 0:4].unsqueeze(2).to_broadcast([128, 4, 64]), ALU.subtract, r=["O32", "GSa"], w=["O32"])
                    TT("vector", o3, o3, GST[:, 4:8].unsqueeze(2).to_broadcast([128, 4, 64]), ALU.mult, r=["O32", "GSb"], w=["O32"])
                    TT("vector", O32, O32, GW, ALU.mult, r=["O32", "GW"], w=["O32"])
                    cs = CS[c % 2]
                    csk = "CS%d" % (c % 2)
                    TT("vector", cs[:, 0, 0:256], O32, GS[:, gb, :], ALU.mult, r=["O32", "GS"], w=[csk])
                    cs_to_catT(cs, csk, gb, 1, 2, 6)

        group_A()
        P.barrier()
        hmode[0] = "load"
        AR.top = GTOP
        for hp in range(2):
            group_B(hp)
            P.barrier()
            AR.top = GTOP
        group_C()
        P.barrier()
        AR.top = GTOP

        DMA("sync", LNG, lnp_d[l, 0, :].partition_broadcast(128), w=["LNG"])
        DMA("sync", LNB, lnp_d[l, 1, :].partition_broadcast(128), w=["LNB"])
        wo0, ko0 = wload(wsrc(w_out[l], 0, 512))
        wo1, ko1 = wload(wsrc(w_out[l], 512, 1024))
        T1s = [AR.f32(D), AR.f32(D)]
        TNs = [AR.f32(D), AR.f32(D)]
        XO = [AR.f32(D), AR.f32(D)]
        LSTs = [AR.f32(16), AR.f32(16)]
        RT = AR.f32(64)
        CTB = [AR.bf(D), AR.bf(D)]

        def op_stage1(b):
            ctb = CTB[b % 2]
            ctk = "CTB%d" % (b % 2)
            T1 = T1s[b % 2]
            t1k = "T1_%d" % (b % 2)
            DMA("sync", ctb, catd[b * 128:(b + 1) * 128, :], r=["catd"], w=[ctk])
            cth, cthk = to_hT(ctb, ctk)
            for dh, (wo, ko) in enumerate(((wo0, ko0), (wo1, ko1))):
                bk = bank("mm")
                for k in range(8):
                    MM(ps[bk][:, :], cth[:, k, :], wo[:, k, :], k == 0, k == 7, r=[cthk] + ko, w=[pk(bk)])
                TT("vector", T1[:, dh * 512:(dh + 1) * 512], ps[bk][:, :], M2[:, dh * 512:(dh + 1) * 512], ALU.mult,
                   r=[pk(bk), "M2"], w=[pk(bk), t1k])

        st2 = {}

        def op_stage2a(b):
            T1 = T1s[b % 2]
            t1k = "T1_%d" % (b % 2)
            xap, xk = xblock(b)
            ln_front(T1, t1k, xap, xk, LSTs[b % 2], "%d" % (b % 2))

        def op_stage2b(b):
            T1 = T1s[b % 2]
            t1k = "T1_%d" % (b % 2)
            if lat:
                xo = XO[b % 2]
                xok = ["XO%d" % (b % 2)]
            else:
                xo = XC[:, b, :]
                xok = ["XC%d" % b]
            ln_back(T1, t1k, xo, xok, TNs[b % 2], LSTs[b % 2], "%d" % (b % 2))
            if lat:
                DMA("gpsimd", xls[b * 128:(b + 1) * 128, :], xo, r=xok, w=["xls"])
            moe_input(xo, xok, S.gb0 + b, HF, HB, kctr, to_hT, RT)

        op_stage1(0)
        op_stage2a(0)
        for b in range(nblk):
            if b + 1 < nblk:
                op_stage1(b + 1)
                op_stage2a(b + 1)
            op_stage2b(b)

    def ln_front(T1, tkey, xap, xk, LST, sfx=""):
        stats = LST[:, 0:12]
        mv = LST[:, 12:14]
        rstd = LST[:, 14:15]
        nmr = LST[:, 15:16]
        STT("vector", T1, xap, ALPHA, T1, ALU.mult, ALU.add, r=[tkey] + xk, w=[tkey])
        P.op("vector", lambda e: e.bn_stats(out=stats[:, 0:6], in_=T1[:, 0:512]), [tkey], [("lnst" + sfx)])
        P.op("vector", lambda e: e.bn_stats(out=stats[:, 6:12], in_=T1[:, 512:1024]), [tkey], [("lnst2" + sfx)])
        P.op("vector", lambda e: e.bn_aggr(out=mv, in_=stats), [("lnst" + sfx), ("lnst2" + sfx)], [("lnmv" + sfx)])
        ACT(rstd, mv[:, 1:2], AF.Ln, r=[("lnmv" + sfx), "EPSB"], w=[("lnrs" + sfx)], bias=EPSB)
        ACT(rstd, rstd, AF.Exp, r=[("lnrs" + sfx)], w=[("lnrs" + sfx)], scale=-0.5)
        STT("vector", nmr, mv[:, 0:1], -1.0, rstd, ALU.mult, ALU.mult, r=[("lnmv" + sfx), ("lnrs" + sfx)], w=[("lnnm" + sfx)])

    def ln_back(T1, tkey, out_ap, out_keys, TN, LST, sfx=""):
        rstd = LST[:, 14:15]
        nmr = LST[:, 15:16]
        ACT(TN, T1, AF.Identity, r=[tkey, ("lnrs" + sfx), ("lnnm" + sfx)], w=[("TN" + sfx)], bias=nmr, scale=rstd)
        TT("vector", TN, TN, LNG, ALU.mult, r=[("TN" + sfx), "LNG"], w=[("TN" + sfx)])
        TT("vector", out_ap, TN, LNB, ALU.add, r=[("TN" + sfx), "LNB"], w=out_keys)

    def moe_input(xo, xok, gb, HF, HB, kctr, to_hT, RT):
        i = kctr[0] % 2
        kctr[0] += 1
        TT("vector", HF, xo, M0, ALU.mult, r=xok + ["M0"], w=["HF"])
        TT("vector", HB[i], HF, M1, ALU.add, r=["HF", "M1"], w=["HB%d" % i])
        DMA("gpsimd", hbuf[gb * 128:(gb + 1) * 128, :], HB[i], r=["HB%d" % i], w=["hbuf"])
        ht, htk = to_hT(HB[i], "HB%d" % i)
        bk = bank("mm")
        for k in range(8):
            MM(ps[bk][:, 0:16], ht[:, k, :], WRB[:, k, :], k == 0, k == 7, r=[htk, "WRB"], w=[pk(bk)])
        NMX = RT[:, 0:1]
        SSUM = RT[:, 1:2]
        EX = RT[:, 16:32]
        RED("vector", NMX, ps[bk][:, 0:16], ALU.max, r=[pk(bk)], w=[pk(bk), "NMX"])
        TS("vector", NMX, NMX, -1.0, None, ALU.mult, None, r=["NMX"], w=["NMX"])
        ACT(EX, ps[bk][:, 0:16], AF.Exp, r=[pk(bk), "NMX"], w=[pk(bk), "EX", "SSUM"], bias=NMX, accum=SSUM)
        P.op("vector", lambda e: e.reciprocal(out=SSUM, in_=SSUM), ["SSUM"], ["SSUM"])
        TS("vector", AFF[:, gb, :], EX, SSUM, None, ALU.mult, None, r=["EX", "SSUM"], w=["AFF"])


    def moe(l, do_lat):
        reqs = [([0, 1], CAP_C), ([2, 3], CAP_C)]
        if do_lat:
            reqs.append((list(range(4, 20)), CAP_L))
        ngb = 20 if do_lat else 4
        nsc = 3 if do_lat else 1
        NS = nsc * 128
        alloc_wsl(6)
        del WST[2:]
        WST.append(MREG[:, 0:2048].rearrange("p (k c) -> p k c", k=8))
        WST.append(MREG[:, 2048:4096].rearrange("p (k c) -> p k c", k=8))
        AT = AR.f32(ngb * 128)
        WK = AR.f32(ngb * 128)
        M8 = AR.f32(8)
        THR = AR.f32(1)
        MASKF = AR.f32(ngb * 16).rearrange("p (g e) -> p g e", g=ngb)
        MASKB = AR.bf(ngb * 16).rearrange("p (g e) -> p g e", g=ngb)
        RANK = AR.f32(ngb * 16).rearrange("p (g e) -> p g e", g=ngb)
        TG = AR.bf(ngb * 16 * 4).rearrange("p (g e f) -> p g e f", g=ngb, e=16)
        TGF = TG.rearrange("p g e f -> p (g e) f")
        AHI = AR.f32(ngb * 16)
        AFFF = AFF[:, 0:ngb, :].rearrange("p g e -> p (g e)")
        ZT1 = AR.f32(D)
        PSC = [AR.bf(4 * 64).rearrange("p (g c) -> p g c", g=4) for _ in range(2)]
        PSL = [AR.bf(16 * 256).rearrange("p (g c) -> p g c", g=16) for _ in range(2)] if do_lat else None
        IDT = [AR.f32(12).rearrange("p (s f) -> p s f", s=3) for _ in range(2)]
        IG = [AR.f32(3) for _ in range(2)]
        GATE = [AR.f32(3) for _ in range(2)]
        IDXG = [AR.i32(3) for _ in range(2)]
        IDXS = [AR.i32(3) for _ in range(2)]
        XE = [AR.bf(3 * D).rearrange("p (s d) -> p s d", s=3) for _ in range(2)]
        XET = AR.bf(8 * 384).rearrange("p (k c) -> p k c", k=8)
        SA = [AR.f32(384), AR.f32(384)]
        ACTT = AR.bf(8 * 384).rearrange("p (k c) -> p k c", k=8)
        YE = [AR.f32(D), AR.f32(D)]

        MS("gpsimd", ZT1, 0.0, ["ZT1"])
        for gb in range(ngb):
            DMA("gpsimd", ysc[gb * 128:(gb + 1) * 128, :], ZT1, r=["ZT1"], w=["ysc"])
        for g0 in range(0, ngb, 4):
            tb_ = bank("tr")
            for i in range(4):
                TR(ps[tb_][0:16, i * 128:(i + 1) * 128], AFF[:, g0 + i, :], IDF, r=["AFF", "CF"], w=[pk(tb_)])
            CP("vector", AT[0:16, g0 * 128:(g0 + 4) * 128], ps[tb_][0:16, 0:512], r=[pk(tb_)], w=[pk(tb_), "AT"])
        for ri, (blks, cap) in enumerate(reqs):
            c0, c1 = blks[0] * 128, (blks[-1] + 1) * 128
            wkk = "WK%d" % ri
            CP("gpsimd", WK[0:16, c0:c1], AT[0:16, c0:c1], r=["AT"], w=[wkk])
            rounds = cap // 8
            for r_ in range(rounds):
                P.op("vector", lambda e, c0=c0, c1=c1: e.max(out=M8[0:16, 0:8], in_=WK[0:16, c0:c1]), [wkk], ["M8"])
                if r_ < rounds - 1:
                    P.op("vector", lambda e, c0=c0, c1=c1: e.match_replace(out=WK[0:16, c0:c1], in_to_replace=M8[0:16, 0:8],
                                                                          in_values=WK[0:16, c0:c1], imm_value=-1.0), ["M8", wkk], [wkk])
            RED("vector", THR[0:16, 0:1], M8[0:16, 0:8], ALU.min, r=["M8"], w=["THR"])
            TS("vector", WK[0:16, c0:c1], AT[0:16, c0:c1], THR[0:16, 0:1], None, ALU.is_ge, None, r=["AT", "THR", wkk], w=[wkk])
        allwk = ["WK%d" % ri for ri in range(len(reqs))]
        mb = bank("mm")
        for gb in range(ngb):
            TR(ps[mb][:, gb * 16:(gb + 1) * 16], WK[0:16, gb * 128:(gb + 1) * 128], IDF[0:16, 0:16], r=allwk + ["CF"], w=[pk(mb)])
        CP("vector", MASKF, ps[mb][:, 0:ngb * 16].rearrange("p (g e) -> p g e", g=ngb), r=[pk(mb)], w=[pk(mb), "MASKF"])
        CP("gpsimd", MASKB, MASKF, r=["MASKF"], w=["MASKB"])
        for (blks, cap) in reqs:
            for bi, gb in enumerate(blks):
                rb = bank("mm")
                for bj in range(bi):
                    MM(ps[rb][:, 0:16], ONESB, MASKB[:, blks[bj], :], bj == 0, False, r=["MASKB", "CB"], w=[pk(rb)])
                MM(ps[rb][:, 0:16], SUB, MASKB[:, gb, :], bi == 0, True, r=["MASKB", "CB"], w=[pk(rb)])
                CP("vector", RANK[:, gb, :], ps[rb][:, 0:16], r=[pk(rb)], w=[pk(rb), "RANK"])
        for gb in range(ngb):
            MS("gpsimd", TG[:, gb, :, 0], float(gb), ["TG0"])
        CP("gpsimd", TGF[:, :, 1], CF[:, C_PLO:C_PLO + 1].to_broadcast([128, ngb * 16]), r=["CF"], w=["TG1"])
        CP("vector", TGF[:, :, 2], AFFF, r=["AFF"], w=["TG2"])
        CP("vector", AHI, TGF[:, :, 2], r=["TG2"], w=["AHI"])
        TT("vector", TGF[:, :, 3], AFFF, AHI, ALU.subtract, r=["AFF", "AHI"], w=["TG3"])
        TGK = ["TG0", "TG1", "TG2", "TG3"]
        for pe in range(2):
            MS("vector", PSC[pe], 0.0, ["PSC%d" % pe])
            MS("vector", IDXG[pe], 0, ["IDXG%d" % pe])
            CP("vector", IDXS[pe][:, 0:1], CF[:, C_PADS:C_PADS + 1], r=["CF"], w=["IDXS%d" % pe])
            MS("vector", GATE[pe], 0.0, ["GATE%d" % pe])
            MS("vector", IG[pe], 0.0, ["IG%d" % pe])
            MS("vector", IDT[pe], 0.0, ["IDT%d" % pe])

        wts = {}

        def wl(e_, name):
            if e_ >= E:
                return
            if name[0] == "D":
                c0 = int(name[1]) * 512
                wts[(e_, name)] = wload(wsrc(w_dn[l, e_], c0, c0 + 512))
            else:
                c0 = (0 if name[0] == "A" else DFF) + int(name[1]) * 512
                wts[(e_, name)] = wload(wsrc(w_gu[l, e_], c0, c0 + 512))

        def prep_a(e_):
            pe = e_ % 2
            sf = "%d" % pe
            for gb in range(4):
                off = 0 if gb < 2 else 32
                TS("vector", PSC[pe][:, gb, off:off + 32], CF[:, C_IOTA:C_IOTA + 32], RANK[:, gb, e_:e_ + 1], MASKF[:, gb, e_:e_ + 1],
                   ALU.is_equal, ALU.mult, r=["CF", "RANK", "MASKF"], w=["PSC" + sf])
            if do_lat:
                for bi in range(16):
                    TS("vector", PSL[pe][:, bi, :], CF[:, C_IOTA:C_IOTA + 256], RANK[:, 4 + bi, e_:e_ + 1], MASKF[:, 4 + bi, e_:e_ + 1],
                       ALU.is_equal, ALU.mult, r=["CF", "RANK", "MASKF"], w=["PSL%s_%d" % (sf, bi)])

        def prep_a2(e_):
            pe = e_ % 2
            sf = "%d" % pe
            ib = bank("mm")
            for gb in range(4):
                MM(ps[ib][0:64, 0:4], PSC[pe][:, gb, :], TG[:, gb, e_, :], gb == 0, gb == 3, r=["PSC" + sf] + TGK, w=[pk(ib)])
            if do_lat:
                for cch in range(2):
                    for bi in range(16):
                        MM(ps[ib][:, 4 + cch * 4:8 + cch * 4], PSL[pe][:, bi, cch * 128:(cch + 1) * 128], TG[:, 4 + bi, e_, :],
                           bi == 0, bi == 15, r=["PSL%s_%d" % (sf, bi)] + TGK, w=[pk(ib)])
            idt, ig, gt, ixg, ixs = IDT[pe], IG[pe], GATE[pe], IDXG[pe], IDXS[pe]
            CP("vector", idt[0:64, 0, :], ps[ib][0:64, 0:4], r=[pk(ib)], w=[pk(ib), "IDT" + sf])
            if do_lat:
                CP("vector", idt[:, 1:3, :], ps[ib][:, 4:12].rearrange("p (s f) -> p s f", s=2), r=[pk(ib)], w=[pk(ib), "IDT" + sf])
            STT("vector", ig[0:64, 0:1], idt[0:64, 0, 0:1], 128.0, idt[0:64, 0, 1:2], ALU.mult, ALU.add, r=["IDT" + sf], w=["IG" + sf])
            TT("vector", gt[0:64, 0:1], idt[0:64, 0, 2:3], idt[0:64, 0, 3:4], ALU.add, r=["IDT" + sf], w=["GATE" + sf])
            if do_lat:
                STT("vector", ig[:, 1:3], idt[:, 1:3, 0], 128.0, idt[:, 1:3, 1], ALU.mult, ALU.add, r=["IDT" + sf], w=["IG" + sf])
                TT("vector", gt[:, 1:3], idt[:, 1:3, 2], idt[:, 1:3, 3], ALU.add, r=["IDT" + sf], w=["GATE" + sf])
            CP("vector", ixg[0:64, 0:1], ig[0:64, 0:1], r=["IG" + sf], w=["IDXG" + sf])
            CP("vector", ixs[0:64, 0:1], ig[0:64, 0:1], r=["IG" + sf], w=["IDXS" + sf])
            if do_lat:
                CP("vector", ixg[:, 1:3], ig[:, 1:3], r=["IG" + sf], w=["IDXG" + sf])
                CP("vector", ixs[:, 1:3], ig[:, 1:3], r=["IG" + sf], w=["IDXS" + sf])
            for sc in range(nsc):
                P.op("gpsimd", lambda e, sc=sc, pe=pe, ixg=ixg: e.indirect_dma_start(
                    out=XE[pe][:, sc, :], out_offset=None, in_=hbuf[0:ngb * 128, :],
                    in_offset=bass.IndirectOffsetOnAxis(ixg[:, sc:sc + 1], 0)),
                    ["IDXG" + sf, "hbuf"], ["XE%s_%d" % (sf, sc)], dma=True)

        def prep_b(e_):
            pe = e_ % 2
            sf = "%d" % pe
            for k in range(8):
                tb_ = bank("tr")
                for sc in range(nsc):
                    TR(psb[tb_][:, sc * 128:(sc + 1) * 128], XE[pe][:, sc, k * 128:(k + 1) * 128], IDB,
                       r=["XE%s_%d" % (sf, sc), "CB"], w=[pk(tb_)])
                CP("vector" if k % 2 else "scalar", XET[:, k, 0:NS], psb[tb_][:, 0:NS], r=[pk(tb_)], w=[pk(tb_), "XET"])

        for name in ("A0", "B0", "A1", "B1", "D0", "D1"):
            wl(0, name)
        prep_a(0)
        prep_a2(0)
        for e_ in range(E):
            pe = e_ % 2
            sf = "%d" % pe
            if e_ + 1 < E:
                prep_a(e_ + 1)
            prep_b(e_)
            for fc in range(8):
                ga, gak = wts[(e_, "A%d" % (fc // 4))]
                gb_, gbk = wts[(e_, "B%d" % (fc // 4))]
                o = (fc % 4) * 128
                pa = bank("mm")
                for k in range(8):
                    MM(ps[pa][:, 0:NS], ga[:, k, o:o + 128], XET[:, k, 0:NS], k == 0, k == 7, r=gak + ["XET"], w=[pk(pa)])
                pb_ = bank("sc")
                for k in range(8):
                    MM(ps[pb_][:, 0:NS], gb_[:, k, o:o + 128], XET[:, k, 0:NS], k == 0, k == 7, r=gbk + ["XET"], w=[pk(pb_)])
                sa = SA[fc % 2]
                sak = "SA%d" % (fc % 2)
                ACT(sa[:, 0:NS], ps[pa][:, 0:NS], AF.Silu, r=[pk(pa)], w=[pk(pa), sak])
                TT("vector", ACTT[:, fc, 0:NS], ps[pb_][:, 0:NS], sa[:, 0:NS], ALU.mult, r=[pk(pb_), sak], w=[pk(pb_), "ACTT%d" % fc])
                if fc == 3:
                    wl(e_ + 1, "A0")
                    wl(e_ + 1, "B0")
                    if e_ + 1 < E:
                        prep_a2(e_ + 1)
                if fc == 7:
                    wl(e_ + 1, "A1")
                    wl(e_ + 1, "B1")
            for sc in range(nsc):
                ye = YE[sc % 2]
                yek = "YE%d" % (sc % 2)
                for dh in range(2):
                    dw, dwk = wts[(e_, "D%d" % dh)]
                    pd_ = bank("mm")
                    for fc in range(8):
                        MM(ps[pd_][:, :], ACTT[:, fc, sc * 128:(sc + 1) * 128], dw[:, fc, :], fc == 0, fc == 7,
                           r=["ACTT%d" % fc] + dwk, w=[pk(pd_)])
                    TS("vector", ye[:, dh * 512:(dh + 1) * 512], ps[pd_][:, :], GATE[pe][:, sc:sc + 1], None, ALU.mult, None,
                       r=[pk(pd_), "GATE" + sf], w=[pk(pd_), yek])
                P.op("gpsimd", lambda e, sc=sc, ye=ye, pe=pe: e.indirect_dma_start(
                    out=ysc[:, :], out_offset=bass.IndirectOffsetOnAxis(IDXS[pe][:, sc:sc + 1], 0),
                    in_=ye, in_offset=None, compute_op=ALU.add),
                    ["IDXS" + sf, yek], ["ysc", yek + "s"], dma=True)
            wl(e_ + 1, "D0")
            wl(e_ + 1, "D1")

    def ln2(l, do_lat, last):
        NB = 4
        T1s = [AR.f32(D) for _ in range(NB)]
        XI_ = [AR.f32(D) for _ in range(NB)]
        LSTs = [AR.f32(16) for _ in range(NB)]
        TNs = [AR.f32(D) for _ in range(NB)]
        XO = [AR.f32(D), AR.f32(D)]
        DMA("sync", LNG, lnp_d[l, 2, :].partition_broadcast(128), w=["LNG"])
        DMA("sync", LNB, lnp_d[l, 3, :].partition_broadcast(128), w=["LNB"])
        get_mod(M2, "M2", 0, l, 5, False)
        for b in range(4):
            i = b % NB
            T1 = T1s[i]
            tk = "T1_%d" % i
            DMA("sync", T1, ysc[b * 128:(b + 1) * 128, :], r=["ysc"], w=[tk])
            TT("vector", T1, T1, M2, ALU.mult, r=[tk, "M2"], w=[tk])
            ln_front(T1, tk, XC[:, b, :], ["XC%d" % b], LSTs[i], "%d" % i)
        for b in range(4):
            i = b % NB
            ln_back(T1s[i], "T1_%d" % i, XC[:, b, :], ["XC%d" % b], TNs[i], LSTs[i], "%d" % i)
            if last:
                DMA("gpsimd", yc[b * 128:(b + 1) * 128, :], XC[:, b, :], r=["XC%d" % b])
        if do_lat:
            get_mod(M2, "M2", 1, l, 5, False)

            def l2_front(b):
                i = b % NB
                T1 = T1s[i]
                tk = "T1_%d" % i
                DMA("sync", T1, ysc[(4 + b) * 128:(5 + b) * 128, :], r=["ysc"], w=[tk])
                DMA("sync", XI_[i], xls[b * 128:(b + 1) * 128, :], r=["xls"], w=["XI%d" % i])
                TT("vector", T1, T1, M2, ALU.mult, r=[tk, "M2"], w=[tk])
                ln_front(T1, tk, XI_[i], ["XI%d" % i], LSTs[i], "%d" % i)

            def l2_back(b):
                i = b % NB
                j = b % 2
                ln_back(T1s[i], "T1_%d" % i, XO[j], ["XO%d" % j], TNs[i], LSTs[i], "%d" % i)
                dst = yl if last else xls
                DMA("gpsimd", dst[b * 128:(b + 1) * 128, :], XO[j], r=["XO%d" % j], w=["xls"])

            SK = 2
            for b in range(SK):
                l2_front(b)
            for b in range(16):
                if b + SK < 16:
                    l2_front(b + SK)
                l2_back(b)

    for l in range(n_layers):
        DMA("sync", ESINK, sink[l, :].partition_broadcast(128), w=["ESINK"])
        ACT(ESINK, ESINK, AF.Exp, r=["ESINK"], w=["ESINK"])
        DMA("sync", GW, gn[l, :].partition_broadcast(128), w=["GW"])
        DMA("sync", WRF, w_r[l].rearrange("(k p) e -> p k e", p=128), w=["WRF"])
        CP("vector", WRB, WRF, r=["WRF"], w=["WRB"])
        mixer(SC_, l)
        P.barrier()
        del WST[2:]
        AR.top = PERSIST_TOP
        if do_lat:
            mixer(SL_, l)
            P.barrier()
            del WST[2:]
            AR.top = PERSIST_TOP
        moe(l, do_lat)
        P.barrier()
        del WST[2:]
        AR.top = PERSIST_TOP
        alloc_wsl(3)
        ln2(l, do_lat, l == n_layers - 1)
        P.barrier()
        AR.top = PERSIST_TOP
    build_nc.last_prog = P
    P.emit(nc, max_ops)
    stack.close()
    return nc


def _rearrange_w_in(w_in):
    qa, ka, va = w_in[:, :, 0:512], w_in[:, :, 512:640], w_in[:, :, 640:768]
    qb, kb, vb = w_in[:, :, 768:1024], w_in[:, :, 1024:1280], w_in[:, :, 1280:1536]
    rest = w_in[:, :, 1536:2560]
    ka0, ka1 = ka[:, :, 0:64], ka[:, :, 64:128]
    cols = [qa, ka0, ka0, ka1, ka1, va]
    for hp in range(2):
        s = slice(hp * 128, (hp + 1) * 128)
        cols += [qb[:, :, s], kb[:, :, s], vb[:, :, s]]
    cols.append(rest)
    out = np.ascontiguousarray(np.concatenate(cols, axis=-1))
    assert out.shape[-1] == NCOL
    return out


_NC_CACHE = {}


def _get_nc(n_layers=DEPTH, do_lat=True):
    key = (n_layers, do_lat)
    if key not in _NC_CACHE:
        _NC_CACHE[key] = build_nc(n_layers, do_lat)
    return _NC_CACHE[key]


def make_in_maps(inputs, n_cores=8):
    f = lambda a: np.ascontiguousarray(np.asarray(a, dtype=np.float32))
    I = {k: f(v) for k, v in inputs.items()}
    w_in_r = _rearrange_w_in(I["w_in"])
    tb = _expand_rpb(I["na_rpb"])
    dec = I["ret_decay"].reshape(DEPTH, 8)
    lnp = np.ascontiguousarray(np.stack([I["ln1_g"], I["ln1_b"], I["ln2_g"], I["ln2_b"]], axis=1))
    cf = _consts()
    ropec, ropes = _rope_tables()
    augk, augq = _aug_tables()
    shared = {
        "w_ada": I["w_ada"], "b_ada": I["b_ada"], "w_in": w_in_r, "w_out": I["w_out"], "w_r": I["w_router"],
        "w_gu": I["w_gate_up"], "w_dn": I["w_down"], "sink": I["attn_sink"], "tb": tb, "dec": np.ascontiguousarray(dec),
        "gn": I["ret_gn"], "lnp": lnp, "cf": cf, "augk": augk, "augq": augq, "ropec": ropec, "ropes": ropes,
    }
    maps = []
    for i in range(n_cores):
        b = i % 4
        m = dict(shared)
        m["xc"] = np.ascontiguousarray(I["x_prompt"][2 * i:2 * i + 2].reshape(512, D))
        m["xl"] = np.ascontiguousarray(I["x_sample"][b])
        m["cond"] = np.ascontiguousarray(np.stack([I["c_ctx"], I["c"][b]], axis=0))
        m["cak"] = np.ascontiguousarray(I["cache_attn_a_k"][b].reshape(DEPTH, PAST, 128))
        m["cav"] = np.ascontiguousarray(I["cache_attn_a_v"][b].reshape(DEPTH, PAST, 128))
        m["cbk"] = np.ascontiguousarray(I["cache_attn_b_k"][b].reshape(DEPTH, PAST, 256))
        m["cbv"] = np.ascontiguousarray(I["cache_attn_b_v"][b].reshape(DEPTH, PAST, 256))
        m["st0"] = np.ascontiguousarray(I["state_ret"][b])
        maps.append(m)
    return maps


def assemble(results):
    y_prompt = np.zeros((16, NCTX, D), np.float32)
    y_sample = np.zeros((4, NL, D), np.float32)
    a_k = np.zeros((16, DEPTH, NCTX, 2, 64), np.float32)
    a_v = np.zeros((16, DEPTH, NCTX, 2, 64), np.float32)
    b_k = np.zeros((16, DEPTH, NCTX, 4, 64), np.float32)
    b_v = np.zeros((16, DEPTH, NCTX, 4, 64), np.float32)
    st = np.zeros((16, DEPTH, 2, 4, 64, 64), np.float32)
    for i, r in enumerate(results):
        y_prompt[2 * i:2 * i + 2] = np.asarray(r["yc"]).reshape(2, NCTX, D)
        if i < 4:
            y_sample[i] = np.asarray(r["yl"])
        a_k[2 * i:2 * i + 2] = np.asarray(r["oak"]).reshape(2, DEPTH, NCTX, 2, 64)
        a_v[2 * i:2 * i + 2] = np.asarray(r["oav"]).reshape(2, DEPTH, NCTX, 2, 64)
        b_k[2 * i:2 * i + 2] = np.asarray(r["obk"]).reshape(2, DEPTH, NCTX, 4, 64)
        b_v[2 * i:2 * i + 2] = np.asarray(r["obv"]).reshape(2, DEPTH, NCTX, 4, 64)
        st[2 * i:2 * i + 2] = np.asarray(r["ost"])
    return (y_prompt, y_sample, a_k, a_v, b_k, b_v, st)


def kernel(**inputs):
    nc = _get_nc()
    maps = make_in_maps(inputs)
    res = run_bass_kernel_spmd(nc, maps, core_ids=list(range(8)))
    return assemble(res.results)
```
